# Optimizing a Trainium2 kernel written in Bass

```python
import math
import jax, jax.numpy as jnp
from jax import lax
import numpy as np

D_MODEL = 2048
BATCH = 2
SEQ = 4096
DEPTH = 1
DEC_BATCH = 8
DEC_SEQ = 4
PAST_LEN = 16384
PAGE_SIZE = 128

HEAD_DIM = 128
MIX_WIDTH = D_MODEL
CONV_CH = D_MODEL // 4
ATTN_WIDTH = MIX_WIDTH - CONV_CH
N_ATTN_HEADS = ATTN_WIDTH // HEAD_DIM
DILATION_GROUPS = ((128, 1), (512, 4), (2048, 16))
HEADS_PER_GROUP = N_ATTN_HEADS // len(DILATION_GROUPS)
CONV_K = 31
FFN_HIDDEN = ((8 * D_MODEL + 2) // 3 + 255) // 256 * 256
ROPE_THETA = 10000.0
DEEPNORM_ALPHA = (2.0 * DEPTH) ** 0.25
DEEPNORM_BETA = (8.0 * DEPTH) ** -0.25
LN_EPS = 1e-5
Q_BLOCK = 128
ATTN_SCALE = HEAD_DIM ** -0.5
W_IN_COLS = 3 * ATTN_WIDTH + 2 * CONV_CH
SPLITS = (ATTN_WIDTH, 2 * ATTN_WIDTH, 3 * ATTN_WIDTH, 3 * ATTN_WIDTH + CONV_CH)

kernel_name = 'hymba_dilated_conformer_deepnorm_step'


def _layer_norm(x, g, b):
    xf = x.astype(jnp.float32)
    mu = jnp.mean(xf, axis=-1, keepdims=True)
    var = jnp.mean(jnp.square(xf - mu), axis=-1, keepdims=True)
    return ((xf - mu) * lax.rsqrt(var + LN_EPS) * g.astype(jnp.float32) + b.astype(jnp.float32)).astype(x.dtype)


def _rope(x, pos):
    half = HEAD_DIM // 2
    inv = ROPE_THETA ** (-jnp.arange(half, dtype=jnp.float32) / half)
    ang = pos.astype(jnp.float32)[:, None] * inv[None, :]
    cos = jnp.cos(ang)[:, None, :]
    sin = jnp.sin(ang)[:, None, :]
    x1 = x[..., :half].astype(jnp.float32)
    x2 = x[..., half:].astype(jnp.float32)
    return jnp.concatenate([x1 * cos - x2 * sin, x2 * cos + x1 * sin], axis=-1).astype(x.dtype)


def _attend(s, v, spec):
    m = jnp.max(s, axis=-1, keepdims=True)
    e = jnp.exp(s - m)
    den = jnp.sum(e, axis=-1, keepdims=True)
    out = jnp.einsum(spec, (e / den).astype(v.dtype), v)
    lse = (m + jnp.log(den))[..., 0]
    return out, lse


def _dilated_prompt(q, k, v, window, dilation):
    B, S, H, Dh = q.shape
    L = S // dilation
    nk = window // dilation
    n_blk = -(-L // Q_BLOCK)
    Lp = n_blk * Q_BLOCK

    def to_classes(t):
        return t.reshape(B, L, dilation, H, Dh).transpose(0, 2, 1, 3, 4)

    qb = jnp.pad(to_classes(q), ((0, 0), (0, 0), (0, Lp - L), (0, 0), (0, 0)))
    qb = qb.reshape(B, dilation, n_blk, Q_BLOCK, H, Dh)
    kpad = ((0, 0), (0, 0), (nk, Lp - L), (0, 0), (0, 0))
    kc = jnp.pad(to_classes(k), kpad)
    vc = jnp.pad(to_classes(v), kpad)
    blk = jnp.arange(n_blk)[:, None]
    idx = blk * Q_BLOCK + jnp.arange(Q_BLOCK + nk)[None, :]
    kb = kc[:, :, idx]
    vb = vc[:, :, idx]
    s = jnp.einsum('brnqhe,brnkhe->brnhqk', qb, kb).astype(jnp.float32) * ATTN_SCALE
    qq = jnp.arange(Q_BLOCK)[:, None]
    kk = jnp.arange(Q_BLOCK + nk)[None, :]
    dist = qq - kk + nk
    kidx = blk[:, :, None] * Q_BLOCK + kk[None] - nk
    valid = (dist >= 0)[None] & (dist <= nk)[None] & (kidx >= 0)
    s = jnp.where(valid[None, None, :, None], s, -jnp.inf)
    out, lse = _attend(s, vb, 'brnhqk,brnkhe->brnqhe')
    out = out.reshape(B, dilation, Lp, H, Dh)[:, :, :L].transpose(0, 2, 1, 3, 4).reshape(B, S, H, Dh)
    lse = lse.transpose(0, 1, 2, 4, 3).reshape(B, dilation, Lp, H)[:, :, :L]
    lse = lse.transpose(0, 2, 1, 3).reshape(B, S, H)
    return out, lse


def _dilated_sample(q, k_hist, v_hist, window, dilation):
    B, T, H, Dh = q.shape
    buf = k_hist.shape[1] - T
    nk = window // dilation
    idx = buf + jnp.arange(T)[:, None] - dilation * jnp.arange(nk + 1)[None, :]
    valid = idx >= 0
    idx = jnp.maximum(idx, 0)
    kg = k_hist[:, idx]
    vg = v_hist[:, idx]
    s = jnp.einsum('bthe,btjhe->bhtj', q, kg).astype(jnp.float32) * ATTN_SCALE
    s = jnp.where(valid[None, None], s, -jnp.inf)
    out, lse = _attend(s, vg, 'bhtj,btjhe->bthe')
    return out, lse.transpose(0, 2, 1)


def _layer(x, pos, kv_bufs, conv_buf, w_in, w_out, conv_w, conv_b, conv_ln_g, conv_ln_b,
           ln1_g, ln1_b, w_gate, w_up, w_down, ln2_g, ln2_b):
    B, T, _ = x.shape
    h = jnp.einsum('btd,de->bte', x, w_in)
    q, k, v, a, g = jnp.split(h, SPLITS, axis=-1)
    q = _rope(q.reshape(B, T, N_ATTN_HEADS, HEAD_DIM), pos)
    k = _rope(k.reshape(B, T, N_ATTN_HEADS, HEAD_DIM), pos)
    v = v.reshape(B, T, N_ATTN_HEADS, HEAD_DIM)

    outs, lses, new_kv = [], [], []
    for gi, (window, dilation) in enumerate(DILATION_GROUPS):
        hs = slice(gi * HEADS_PER_GROUP, (gi + 1) * HEADS_PER_GROUP)
        qg, kg, vg = q[:, :, hs], k[:, :, hs], v[:, :, hs]
        kv_new = jnp.stack([kg, vg], axis=2)
        if kv_bufs is None:
            o, l = _dilated_prompt(qg, kg, vg, window, dilation)
            keep = min(window, T)
            new_kv.append(kv_new[:, T - keep:])
        else:
            kv_hist = jnp.concatenate([kv_bufs[gi], kv_new], axis=1)
            o, l = _dilated_sample(qg, kv_hist[:, :, 0], kv_hist[:, :, 1], window, dilation)
            new_kv.append(kv_hist[:, T:])
        outs.append(o)
        lses.append(l)
    alpha = jax.nn.softmax(jnp.stack(lses, axis=0), axis=0)
    attn = jnp.concatenate([o * alpha[i][..., None].astype(o.dtype) for i, o in enumerate(outs)], axis=2)
    attn = attn.reshape(B, T, ATTN_WIDTH)

    u = a * jax.nn.sigmoid(g)
    if conv_buf is None:
        conv_buf = jnp.zeros((B, CONV_K - 1, CONV_CH), u.dtype)
    u_hist = jnp.concatenate([conv_buf, u], axis=1)
    c = lax.conv_general_dilated(u_hist, conv_w[:, None, :].astype(u.dtype), (1,), 'VALID',
                                 dimension_numbers=('NWC', 'WIO', 'NWC'),
                                 feature_group_count=CONV_CH) + conv_b
    c = jax.nn.silu(_layer_norm(c, conv_ln_g, conv_ln_b))
    new_conv = u_hist[:, -(CONV_K - 1):]

    mix = jnp.einsum('bte,ed->btd', jnp.concatenate([attn, c], axis=-1), w_out)
    x1 = _layer_norm(DEEPNORM_ALPHA * x + mix, ln1_g, ln1_b)
    f = jax.nn.silu(jnp.einsum('btd,df->btf', x1, w_gate)) * jnp.einsum('btd,df->btf', x1, w_up)
    f = jnp.einsum('btf,fd->btd', f, w_down)
    x2 = _layer_norm(DEEPNORM_ALPHA * x1 + f, ln2_g, ln2_b)
    return x2, new_kv, new_conv


def setup_inputs(seed: int = 0) -> dict:
    key = jax.random.key(seed)
    ks = jax.random.split(key, 20)
    f32 = jnp.float32
    nrm = lambda k, shape: jax.random.normal(k, shape, f32)
    bufs = [min(w, PAST_LEN) for (w, _) in DILATION_GROUPS]
    kv_shape = lambda n: (DEPTH, DEC_BATCH, n, 2, HEADS_PER_GROUP, HEAD_DIM)
    col_scale = jnp.concatenate([jnp.ones((2 * ATTN_WIDTH,), f32),
                                 jnp.full((ATTN_WIDTH + CONV_CH,), DEEPNORM_BETA, f32),
                                 jnp.ones((CONV_CH,), f32)])
    return {
        'x_prompt': nrm(ks[0], (BATCH, SEQ, D_MODEL)),
        'x_sample': nrm(ks[1], (DEC_BATCH, DEC_SEQ, D_MODEL)),
        'cache_kv_w128': nrm(ks[2], kv_shape(bufs[0])),
        'cache_kv_w512': nrm(ks[3], kv_shape(bufs[1])),
        'cache_kv_w2048': nrm(ks[4], kv_shape(bufs[2])),
        'state_conv': 0.5 * nrm(ks[5], (DEPTH, DEC_BATCH, CONV_K - 1, CONV_CH)),
        'w_in': nrm(ks[6], (DEPTH, D_MODEL, W_IN_COLS)) * D_MODEL ** -0.5 * col_scale,
        'w_out': nrm(ks[7], (DEPTH, MIX_WIDTH, D_MODEL)) * MIX_WIDTH ** -0.5 * DEEPNORM_BETA,
        'conv_w': nrm(ks[8], (DEPTH, CONV_K, CONV_CH)) * CONV_K ** -0.5,
        'conv_b': 0.01 * nrm(ks[9], (DEPTH, CONV_CH)),
        'conv_ln_g': 1.0 + 0.01 * nrm(ks[10], (DEPTH, CONV_CH)),
        'conv_ln_b': 0.01 * nrm(ks[11], (DEPTH, CONV_CH)),
        'ln1_g': 1.0 + 0.01 * nrm(ks[12], (DEPTH, D_MODEL)),
        'ln1_b': 0.01 * nrm(ks[13], (DEPTH, D_MODEL)),
        'w_gate': nrm(ks[14], (DEPTH, D_MODEL, FFN_HIDDEN)) * D_MODEL ** -0.5 * DEEPNORM_BETA,
        'w_up': nrm(ks[15], (DEPTH, D_MODEL, FFN_HIDDEN)) * D_MODEL ** -0.5 * DEEPNORM_BETA,
        'w_down': nrm(ks[16], (DEPTH, FFN_HIDDEN, D_MODEL)) * FFN_HIDDEN ** -0.5 * DEEPNORM_BETA,
        'ln2_g': 1.0 + 0.01 * nrm(ks[17], (DEPTH, D_MODEL)),
        'ln2_b': 0.01 * nrm(ks[18], (DEPTH, D_MODEL)),
    }


def reference(x_prompt, x_sample, cache_kv_w128, cache_kv_w512, cache_kv_w2048, state_conv,
              w_in, w_out, conv_w, conv_b, conv_ln_g, conv_ln_b, ln1_g, ln1_b,
              w_gate, w_up, w_down, ln2_g, ln2_b):
    pos_p = jnp.arange(x_prompt.shape[1], dtype=jnp.int32)
    pos_s = PAST_LEN + jnp.arange(x_sample.shape[1], dtype=jnp.int32)
    xp, xs = x_prompt, x_sample
    kvp = [[], [], []]
    kvs = [[], [], []]
    convp, convs = [], []
    for l in range(DEPTH):
        params = (w_in[l], w_out[l], conv_w[l], conv_b[l], conv_ln_g[l], conv_ln_b[l],
                  ln1_g[l], ln1_b[l], w_gate[l], w_up[l], w_down[l], ln2_g[l], ln2_b[l])
        xp, nkv_p, nconv_p = _layer(xp, pos_p, None, None, *params)
        bufs = (cache_kv_w128[l], cache_kv_w512[l], cache_kv_w2048[l])
        xs, nkv_s, nconv_s = _layer(xs, pos_s, bufs, state_conv[l], *params)
        for gi in range(len(DILATION_GROUPS)):
            kvp[gi].append(nkv_p[gi])
            kvs[gi].append(nkv_s[gi])
        convp.append(nconv_p)
        convs.append(nconv_s)
    y_prompt, y_sample = xp, xs
    new_kv_w128_prompt = jnp.stack(kvp[0], axis=0)
    new_kv_w512_prompt = jnp.stack(kvp[1], axis=0)
    new_kv_w2048_prompt = jnp.stack(kvp[2], axis=0)
    new_conv_prompt = jnp.stack(convp, axis=0)
    new_kv_w128_sample = jnp.stack(kvs[0], axis=0)
    new_kv_w512_sample = jnp.stack(kvs[1], axis=0)
    new_kv_w2048_sample = jnp.stack(kvs[2], axis=0)
    new_conv_sample = jnp.stack(convs, axis=0)
    return (y_prompt, y_sample, new_kv_w128_prompt, new_kv_w512_prompt, new_kv_w2048_prompt, new_conv_prompt,
            new_kv_w128_sample, new_kv_w512_sample, new_kv_w2048_sample, new_conv_sample)
```

```python
import numpy as np
from contextlib import ExitStack
import concourse.bass as bass
import concourse.mybir as mybir
from concourse.bass_utils import run_bass_kernel_spmd

F32 = mybir.dt.float32
BF16 = mybir.dt.bfloat16
U8 = mybir.dt.uint8
AF = mybir.ActivationFunctionType
ALU = mybir.AluOpType

D = 2048
KT = 16
TOK = 1024
TS = 1028
NTILE = 8
FF = 5632
NFC = 44
HALO = 2048
PAST = 16384
ALPHA = 2.0 ** 0.25
EPS = 1e-5
SCALE = 128.0 ** -0.5
NEG = -30000.0
NCORES = 8
TAB_G = {0: 0, 1: 9, 2: 21}
TAB_S = 45
NTAB = 46
NMASK = 13


class Sched:
    def __init__(self, nc):
        self.nc = nc
        self.ops = []

    def op(self, eng, fn, r=(), w=()):
        self.ops.append(dict(eng=eng, fn=fn, r=tuple(r), w=tuple(w), dma=None, bar=False))

    def dma(self, q, stream, fn, r=(), w=()):
        self.ops.append(dict(eng=q, fn=fn, r=tuple(r), w=tuple(w), dma=stream, bar=False))

    EXCL = ('W0', 'W1', 'Wo2', 'Wo3', 'd2d')

    def barrier(self):
        for e in ('pe', 'act', 'dve', 'pool', 'sp'):
            self.ops.append(dict(eng=e, fn=None, r=(), w=(), dma=None, bar=True))

    def emit(self, stack):
        nc = self.nc
        ops = self.ops
        n = len(ops)
        last_w = {}
        readers = {}
        deps = [None] * n
        last_eng = {}
        last_stream = {}
        i = 0
        while i < n:
            o = ops[i]
            if o['bar']:
                d = set(last_eng.values()) | set(v for k, v in last_stream.items() if k not in self.EXCL)
                j = i
                while j < n and ops[j]['bar']:
                    deps[j] = set(d)
                    j += 1
                last_w = {k: v for k, v in last_w.items() if ops[v]['dma'] in self.EXCL}
                readers = {}
                i = j
                continue
            d = set()
            for k in o['r']:
                if k in last_w:
                    d.add(last_w[k])
            for k in o['w']:
                if k in last_w:
                    d.add(last_w[k])
                for j in readers.get(k, ()):
                    d.add(j)
            d.discard(i)
            deps[i] = d
            for k in o['w']:
                last_w[k] = i
                readers[k] = []
            for k in o['r']:
                readers.setdefault(k, []).append(i)
            if o['dma'] is not None:
                last_stream[o['dma']] = i
            else:
                last_eng[o['eng']] = i
            i += 1
        signal = [False] * n
        for i, o in enumerate(ops):
            keep = set()
            for j in deps[i]:
                pj = ops[j]
                if (not o['bar']) and pj['dma'] is None and pj['eng'] == 'pe' and o['eng'] == 'pe' and o['dma'] is None:
                    continue
                if o['bar'] and pj['dma'] is None and pj['eng'] == o['eng'] and o['eng'] == 'pe':
                    continue
                keep.add(j)
                signal[j] = True
            deps[i] = keep
        cnt = {}
        semname = [None] * n
        semval = [0] * n
        for i, o in enumerate(ops):
            if o['bar']:
                continue
            if o['dma'] is not None:
                s = 'dma_' + o['dma']
                cnt[s] = cnt.get(s, 0) + 16
                semname[i] = s
                semval[i] = cnt[s]
            elif signal[i]:
                s = 'eng_' + o['eng']
                cnt[s] = cnt.get(s, 0) + 1
                semname[i] = s
                semval[i] = cnt[s]
        sems = {}
        for s in cnt:
            sems[s] = stack.enter_context(nc.semaphore(s))
        block = stack.enter_context(nc.Block())
        engs = {'pe': block.tensor, 'act': block.scalar, 'dve': block.vector,
                'pool': block.gpsimd, 'sp': block.sync}

        def make(engname):
            def body(e):
                waited = {}
                for i, o in enumerate(ops):
                    if o['eng'] != engname:
                        continue
                    need = {}
                    for j in deps[i]:
                        s = semname[j]
                        v = semval[j]
                        if waited.get(s, 0) >= v:
                            continue
                        need[s] = max(need.get(s, 0), v)
                    for s, v in need.items():
                        e.wait_ge(sems[s], v)
                        waited[s] = v
                    if o['fn'] is None:
                        continue
                    ins = o['fn'](e)
                    if semname[i] is not None:
                        ins.then_inc(sems[semname[i]], 16 if o['dma'] is not None else 1)
                if engname == 'sp':
                    for s, v in cnt.items():
                        if waited.get(s, 0) < v:
                            e.wait_ge(sems[s], v)
            return body
        for name, dec in engs.items():
            dec(make(name))


class Arena:
    def __init__(self):
        self.bufs = []

    def add(self, name, nbytes, first, last):
        self.bufs.append([name, (nbytes + 63) // 64 * 64, first, last, None])

    def solve(self):
        import random
        best = None
        keys = [lambda b: (-b[1],), lambda b: (-(b[3] - b[2]), -b[1]), lambda b: (b[2], -b[1]), lambda b: (-(b[1] * (b[3] - b[2] + 1)),),
                lambda b: (-(b[3] - b[2] + 1) * 4096 - b[1],)]
        orders = [sorted(self.bufs, key=k) for k in keys]
        rnd = random.Random(0)
        for _ in range(300):
            o = sorted(self.bufs, key=lambda b: (-(b[1] * (b[3] - b[2] + 1)) * (0.5 + rnd.random()),))
            orders.append(o)
        for order in orders:
            placed = []
            offs = {}
            for b in order:
                conflicts = sorted([(offs[p[0]], offs[p[0]] + p[1]) for p in placed if not (p[3] < b[2] or b[3] < p[2])])
                off = 0
                for s_, e_ in conflicts:
                    if off + b[1] <= s_:
                        break
                    off = max(off, e_)
                offs[b[0]] = off
                placed.append(b)
            size = max(offs[b[0]] + b[1] for b in self.bufs)
            if best is None or size < best[0]:
                best = (size, dict(offs))
        self.size, self.off = best
        return self.size


SKIP = set()


def build_program(stop=None):
    nc = bass.Bass("TRN2", target_bir_lowering=False)

    def din(name, shape, dt=F32):
        return nc.dram_tensor(name, list(shape), dt, kind="ExternalInput").ap()

    def dout(name, shape, dt=F32):
        return nc.dram_tensor(name, list(shape), dt, kind="ExternalOutput").ap()

    x_own = din("x_own", [TOK, D])
    x_halo = din("x_halo", [HALO, D])
    x_s = din("x_s", [4, D])
    kvc_in = [din("kvc0", [128, 2, 512]), din("kvc1", [512, 2, 512]), din("kvc2", [2048, 2, 512])]
    sconv = din("sconv", [30, 512])
    w_in = din("w_in", [D, FF])
    w_out = din("w_out", [D, D])
    w_gate = din("w_gate", [D, FF])
    w_up = din("w_up", [D, FF])
    w_down = din("w_down", [FF, D])
    conv_wT = din("conv_wT", [512, 31])
    cvec = din("cvec", [128, 12])
    lnp = din("lnp", [4, D])
    ident_d = din("ident", [128, 128])
    masks_d = din("masks", [128, NMASK, 128])
    rope_d = din("rope", [NTAB, 128, 256])

    y_out = dout("y", [TS, D])
    kvout = [dout(f"kvout{g}", [TOK, 2, 512]) for g in range(3)]
    kvs_out = [dout("kvs0", [128, 2, 512]), dout("kvs1", [512, 2, 512]), dout("kvs2", [2048, 2, 512])]
    conv_out = dout("convout", [30, 512])
    convs_out = dout("convs", [30, 512])
    x1s = nc.dram_tensor("x1s", [TS, D], F32).ap()
    y2s = nc.dram_tensor("y2s", [TS, D], F32).ap()

    PH = dict(init=0, halo=1, xt=2, conv=3, conv2=3.5, g2p=4, g2a=5, g1p=6, g1a=7, g0p=8, g0a=9, fin=10, wout=11, ffn=12, wdn=13, ln2=14)
    LASTPH = 14
    A = Arena()
    A.add('ident_f', 512, 0, LASTPH)
    A.add('ident_b', 256, 0, LASTPH)
    A.add('ones_b', 256, 0, LASTPH)
    A.add('ones_f', 512, 0, LASTPH)
    A.add('masks', NMASK * 128 * 2, 0, LASTPH)
    A.add('cw', 4 * 31 * 4, 0, LASTPH)
    A.add('cvec', 12 * 4, 0, LASTPH)
    A.add('W0', 16384, 0, LASTPH)
    A.add('W1', 16384, 0, LASTPH)
    A.add('XTH', KT * HALO * 2, PH['halo'], PH['halo'])
    A.add('kT2h', 4 * 2048 * 2, PH['halo'], PH['g2a'])
    A.add('V2h', 16 * 512 * 2, PH['halo'], PH['g2a'])
    A.add('xb0', D * 2, PH['halo'], PH['g1p'])
    A.add('xb1', D * 2, PH['halo'], PH['xt'])
    for i in range(2):
        A.add(f'tab{i}', 256 * 4, PH['halo'], PH['g0a'])
        A.add(f'kf{i}', 512 * 4, PH['halo'], PH['g0a'])
        A.add(f'qb{i}', 512 * 2, PH['halo'], PH['g0a'])
    A.add('qb2', 512 * 2, PH['halo'], PH['g0a'])
    A.add('tA', 512 * 4, PH['halo'], PH['g0a'])
    A.add('tB', 512 * 4, PH['halo'], PH['g0a'])
    A.add('XTO', KT * TOK * 2, PH['xt'], PH['g0p'])
    A.add('XTS', KT * 4 * 2, PH['xt'], PH['g0p'])
    A.add('XTH1', KT * 512 * 2, PH['g1p'], PH['g0p'])
    A.add('XTHc', KT * 128 * 2, PH['xt'], PH['conv'])
    A.add('xbs', D * 2, PH['xt'], PH['xt'])
    A.add('uT', 4 * 1054 * 4, PH['conv'], PH['conv'])
    A.add('uTs', 4 * 34 * 4, PH['conv'], PH['conv'])
    A.add('uTb', 4 * 1054 * 2, PH['conv'], PH['conv2'])
    A.add('uTsb', 4 * 34 * 2, PH['conv'], PH['conv2'])
    A.add('cT', 4 * TS * 4, PH['conv2'], PH['conv2'])
    A.add('sq', 4 * 512 * 4, PH['conv2'], PH['conv2'])
    A.add('scv', 512 * 4, PH['conv'], PH['conv'])
    A.add('cvo', 512 * 4, PH['conv'], PH['conv'])
    A.add('cvo2', 512 * 4, PH['conv'], PH['conv'])
    for i in range(2):
        A.add(f'sg{i}', 512 * 4, PH['conv'], PH['conv'])
    for i in range(4):
        A.add(f'dg{i}', 31 * 128 * 2, PH['conv'], PH['conv2'])
    for nm in ('mean', 'm2', 'rstd', 'yt'):
        A.add('cv_' + nm, 512 * 4, PH['conv2'], PH['conv2'])
    A.add('mixTc', 4 * TS * 2, PH['conv2'], PH['wout'])
    A.add('Z', 4 * TS * 4, PH['g2p'], PH['fin'])
    A.add('OT2', 4 * TS * 4, PH['g2p'], PH['fin'])
    A.add('OT1', 4 * TS * 4, PH['g1a'], PH['fin'])
    A.add('OT0', 4 * TS * 4, PH['g0a'], PH['fin'])
    A.add('qT2', 4 * TS * 2, PH['g2p'], PH['g2a'])
    A.add('kT2o', 4 * TS * 2, PH['g2p'], PH['g2a'])
    A.add('V2o', 9 * 512 * 2, PH['g2p'], PH['g2a'])
    A.add('qT1', 4 * TS * 2, PH['g1p'], PH['g1a'])
    A.add('kT1o', 4 * TS * 2, PH['g1p'], PH['g1a'])
    A.add('V1o', 9 * 512 * 2, PH['g1p'], PH['g1a'])
    A.add('kT1h', 4 * 512 * 2, PH['g1p'], PH['g1a'])
    A.add('V1h', 4 * 512 * 2, PH['g1p'], PH['g1a'])
    A.add('qT0', 4 * TS * 2, PH['g0p'], PH['g0a'])
    A.add('kT0o', 4 * TS * 2, PH['g0p'], PH['g0a'])
    A.add('V0o', 9 * 512 * 2, PH['g0p'], PH['g0a'])
    A.add('kT0h', 4 * 128 * 2, PH['g0p'], PH['g0a'])
    A.add('V0h', 1 * 512 * 2, PH['g0p'], PH['g0a'])
    for i in range(4):
        A.add(f'ET{i}', 384 * 2, PH['g2a'], PH['g0a'])
    for i in range(4):
        A.add(f'kvcb{i}', 1024 * 2, PH['g2a'], PH['g0a'])
        A.add(f'kTc{i}', 512 * 2, PH['g2a'], PH['g0a'])
    A.add('mixTa', 12 * TS * 2, PH['fin'], PH['wout'])
    for i in range(3):
        A.add(f'x1p{i}', D * 4, PH['wout'], PH['wout'])
    A.add('Wo2', 16384, PH['g0a'], PH['wout'])
    A.add('Wo3', 16384, PH['g0a'], PH['wout'])
    A.add('x1T', KT * TS * 2, PH['wout'], PH['ffn'])
    A.add('lng', D * 4, PH['wout'], PH['wout'])
    A.add('lnb', D * 4, PH['wout'], PH['wout'])
    for i in range(2):
        A.add(f'xp{i}', D * 4, PH['wout'], PH['wout'])
        A.add(f'x1b{i}', D * 2, PH['wout'], PH['wout'])
    A.add('stat', 18 * 8 * 4, PH['wout'], PH['ln2'])
    A.add('ls1', 9 * 8 * 4, PH['wdn'], PH['ln2'])
    A.add('ls2', 9 * 8 * 4, PH['wdn'], PH['ln2'])
    A.add('ljunk', 256 * 2, PH['wdn'], PH['wdn'])
    A.add('ljunk2', 256 * 2, PH['wdn'], PH['wdn'])
    A.add('fT', NFC * 1152 * 2, PH['ffn'], PH['wdn'])
    for i in range(2):
        A.add(f'sgf{i}', 512 * 4, PH['ffn'], PH['ffn'])
    for i in range(2):
        A.add(f'dxb{i}', 9 * 256 * 4, PH['wdn'], PH['wdn'])
        A.add(f'dyb{i}', 9 * 256 * 4, PH['wdn'], PH['wdn'])
    A.add('lng2', D * 4, PH['wdn'], PH['ln2'])
    A.add('lnb2', D * 4, PH['wdn'], PH['ln2'])
    for i in range(5):
        A.add(f'yt{i}', D * 4, PH['ln2'], PH['ln2'])
    A.add('lnjunk2', D * 4, PH['ln2'], PH['ln2'])
    asize = A.solve()
    assert asize <= 212800, asize

    st = ExitStack()
    with st:
        arena = st.enter_context(nc.sbuf_tensor("arena", [128, asize], U8))

        def buf(name, shape, dt, parts=128):
            nb = int(np.prod(shape)) * (4 if dt == F32 else 2)
            off = A.off[name]
            v = arena[0:parts, off:off + nb].bitcast(dt)
            if len(shape) == 2:
                v = v.rearrange("p (a b) -> p a b", a=shape[0])
            elif len(shape) == 3:
                v = v.rearrange("p (a b c) -> p a b c", a=shape[0], b=shape[1])
            return v

        accs = [st.enter_context(nc.psum_tensor(f"acc{i}", [128, 512], F32)) for i in range(4)]
        pTs = [st.enter_context(nc.psum_tensor(f"pT{i}", [128, 1024], BF16)) for i in range(2)]
        pSs = [st.enter_context(nc.psum_tensor(f"pS{i}", [128, 512], F32)) for i in range(2)]

        ident_f = buf('ident_f', [128], F32)
        ident_b = buf('ident_b', [128], BF16)
        ones_b = buf('ones_b', [128], BF16)
        ones_f = buf('ones_f', [128], F32)
        masks = buf('masks', [NMASK, 128], BF16)
        cw = buf('cw', [4, 31], F32)
        cv = buf('cvec', [12], F32)
        Wb = [buf('W0', [8192], BF16), buf('W1', [8192], BF16)]
        tabs = [buf(f'tab{i}', [256], F32) for i in range(2)]
        kfs = [buf(f'kf{i}', [512], F32) for i in range(2)]
        qbs = [buf(f'qb{i}', [512], BF16) for i in range(3)]
        tA = buf('tA', [512], F32)
        tB = buf('tB', [512], F32)

        S = Sched(nc)
        ctr = dict(acc=0, pT=0, tab=0, kf=0, qb=0, w=0, xb=0)

        def nxt(k, m):
            v = ctr[k] % m
            ctr[k] += 1
            return v

        S.dma('sp', 'c_id', lambda e: e.dma_start(out=ident_f, in_=ident_d), w=['ident_f'])
        S.dma('sp', 'c_cw', lambda e: e.dma_start(out=cw, in_=conv_wT.rearrange("(c p) j -> p c j", p=128)), w=['cw'])
        S.dma('sp', 'c_cv', lambda e: e.dma_start(out=cv, in_=cvec), w=['cvec'])
        S.dma('pool', 'c1', lambda e: e.dma_start(out=masks, in_=masks_d), w=['masks'])
        S.op('dve', lambda e: e.tensor_copy(ident_b, ident_f), r=['ident_f'], w=['ident_b'])
        S.op('dve', lambda e: e.memset(ones_b, 1.0), w=['ones_b'])
        S.op('dve', lambda e: e.memset(ones_f, 1.0), w=['ones_f'])

        def load_w(src3, shape3):
            slot = nxt('w', 2)
            a, b = shape3
            v = Wb[slot][:, 0:a * b].rearrange("p (a b) -> p a b", a=a)
            key = f'W{slot}'
            S.dma('pool', key, lambda e: e.dma_start(out=v, in_=src3), w=[key])
            return v, key

        def build_xt(src_rows, M, xbuf, xkey, dest_fn, dkey, stream):
            S.dma('pool', stream, lambda e: e.dma_start(out=xbuf[0:M, :], in_=src_rows), w=[xkey])
            for half in range(2):
                pi = nxt('pT', 2)
                pT = pTs[pi]
                for j in range(8):
                    kt = half * 8 + j
                    S.op('pe', lambda e, kt=kt, j=j, pT=pT: e.transpose(pT[:, j * 128:j * 128 + M], xbuf[0:M, kt * 128:(kt + 1) * 128], ident_b[0:M, 0:M]),
                         r=[xkey, 'ident_b'], w=[f'pT{pi}'])
                src = pT[:, :].rearrange("p (a b) -> p a b", a=8)[:, :, 0:M]
                dst = dest_fn(half * 8)
                if half == 0:
                    S.op('act', lambda e, src=src, dst=dst: e.copy(out=dst, in_=src), r=[f'pT{pi}'], w=[dkey])
                else:
                    S.op('dve', lambda e, src=src, dst=dst: e.tensor_copy(dst, src), r=[f'pT{pi}'], w=[dkey])

        def proj_tm(lhsT_fn, M, Wv, wkey, xkeys, ncols=512):
            ai = nxt('acc', 4)
            acc = accs[ai]
            for kt in range(KT):
                S.op('pe', lambda e, kt=kt, acc=acc: e.matmul(acc[0:M, 0:ncols], lhsT=lhsT_fn(kt), rhs=Wv[:, kt, 0:ncols], start=(kt == 0), stop=(kt == KT - 1)),
                     r=list(xkeys) + [wkey], w=[f'acc{ai}'])
            return acc, f'acc{ai}'

        def rope(acc, akey, M, tabidx, out_f32, out_bf, okeys):
            ti = nxt('tab', 2)
            tab = tabs[ti]
            S.dma('sp', f'tab{ti}', lambda e: e.dma_start(out=tab, in_=rope_d[tabidx]), w=[f'tab{ti}'])
            a3 = acc[0:M, :].rearrange("p (h d) -> p h d", h=4)
            tA3 = tA[0:M, :].rearrange("p (h d) -> p h d", h=4)
            tB3 = tB[0:M, :].rearrange("p (h d) -> p h d", h=4)
            cos2 = tab[0:M, 0:128].unsqueeze(1).to_broadcast([M, 4, 128])
            sa = tab[0:M, 128:192].unsqueeze(1).to_broadcast([M, 4, 64])
            sb = tab[0:M, 192:256].unsqueeze(1).to_broadcast([M, 4, 64])
            S.op('dve', lambda e: e.tensor_tensor(tA3, a3, cos2, op=ALU.mult), r=[akey, f'tab{ti}'], w=['tA'])
            S.op('dve', lambda e: e.tensor_tensor(tB3[:, :, 0:64], a3[:, :, 64:128], sa, op=ALU.mult), r=[akey, f'tab{ti}'], w=['tB'])
            S.op('dve', lambda e: e.tensor_tensor(tB3[:, :, 64:128], a3[:, :, 0:64], sb, op=ALU.mult), r=[akey, f'tab{ti}'], w=['tB'])
            if out_f32 is not None:
                S.op('pool', lambda e: e.tensor_tensor(out_f32, tA[0:M, :], tB[0:M, :], op=ALU.add), r=['tA', 'tB'], w=[okeys[0]])
                S.op('act', lambda e: e.copy(out=out_bf, in_=out_f32), r=[okeys[0]], w=[okeys[1]])
            else:
                S.op('pool', lambda e: e.tensor_tensor(out_bf, tA[0:M, :], tB[0:M, :], op=ALU.add), r=['tA', 'tB'], w=[okeys[1]])

        def tr4(src_bf, skey, M, dst, dkey):
            pi = nxt('pT', 2)
            pT = pTs[pi]
            for h in range(4):
                S.op('pe', lambda e, h=h: e.transpose(pT[:, h * 128:h * 128 + M], src_bf[0:M, h * 128:(h + 1) * 128], ident_b[0:M, 0:M]),
                     r=[skey, 'ident_b'], w=[f'pT{pi}'])
            src = pT[:, 0:512].rearrange("p (a b) -> p a b", a=4)[:, :, 0:M]
            S.op('dve', lambda e: e.tensor_copy(dst, src), r=[f'pT{pi}'], w=[dkey])

        deferred = []

        def flush_deferred(keep=0):
            while len(deferred) > keep:
                deferred.pop(0)()

        def q_tile(lhsT_fn, M, Wv, wkey, xkeys, tabidx, dst, dkey):
            acc, akey = proj_tm(lhsT_fn, M, Wv, wkey, xkeys)
            qi = nxt('qb', 3)
            rope(acc, akey, M, tabidx, None, qbs[qi][0:M, :], [None, f'qb{qi}'])
            flush_deferred(1)
            deferred.append(lambda: tr4(qbs[qi], f'qb{qi}', M, dst, dkey))

        def k_tile(lhsT_fn, M, Wv, wkey, xkeys, tabidx, dst, dkey, out_rows):
            acc, akey = proj_tm(lhsT_fn, M, Wv, wkey, xkeys)
            qi = nxt('qb', 3)
            if out_rows is None:
                rope(acc, akey, M, tabidx, None, qbs[qi][0:M, :], [None, f'qb{qi}'])
            else:
                ki = nxt('kf', 2)
                rope(acc, akey, M, tabidx, kfs[ki][0:M, :], qbs[qi][0:M, :], [f'kf{ki}', f'qb{qi}'])
                for (p0, np_, drows) in out_rows:
                    S.dma('pool', f'kf{ki}', lambda e, p0=p0, np_=np_, drows=drows: e.dma_start(out=drows, in_=kfs[ki][p0:p0 + np_, :]), r=[f'kf{ki}'])
            flush_deferred(1)
            deferred.append(lambda: tr4(qbs[qi], f'qb{qi}', M, dst, dkey))

        def v_tile(lhsT_fn, M, Wv, wkey, xkeys, dst_bf, dkey, out_rows):
            acc, akey = proj_tm(lhsT_fn, M, Wv, wkey, xkeys)
            flush_deferred(0)
            if out_rows is None:
                S.op('act', lambda e: e.copy(out=dst_bf, in_=acc[0:M, :]), r=[akey], w=[dkey])
            else:
                ki = nxt('kf', 2)
                S.op('act', lambda e: e.copy(out=kfs[ki][0:M, :], in_=acc[0:M, :]), r=[akey], w=[f'kf{ki}'])
                S.op('dve', lambda e: e.tensor_copy(dst_bf, kfs[ki][0:M, :]), r=[f'kf{ki}'], w=[dkey])
                for (p0, np_, drows) in out_rows:
                    S.dma('pool', f'kf{ki}', lambda e, p0=p0, np_=np_, drows=drows: e.dma_start(out=drows, in_=kfs[ki][p0:p0 + np_, :]), r=[f'kf{ki}'])

        def wblk(col0):
            return w_in[:, col0:col0 + 512].rearrange("(kt p) c -> p kt c", p=128)

        XTH = buf('XTH', [KT, HALO], BF16)
        kT2h = buf('kT2h', [4, 2048], BF16)
        V2h = buf('V2h', [16, 512], BF16)
        xbb = [buf(f'xb{i}', [D], BF16) for i in range(2)]
        for ht in range(16):
            xi = nxt('xb', 2)
            build_xt(x_halo[ht * 128:(ht + 1) * 128, :], 128, xbb[xi], f'xb{xi}',
                     lambda kt0, ht=ht: XTH[:, kt0:kt0 + 8, ht * 128:(ht + 1) * 128], 'XTH', f'xb{xi}')
        Wk, wk_key = load_w(wblk(1536 + 1024), (KT, 512))
        Wv_, wv_key = load_w(wblk(3072 + 1024), (KT, 512))
        for r in range(16):
            k_tile(lambda kt, r=r: XTH[:, kt, r:HALO:16], 128, Wk, wk_key, ['XTH'], TAB_G[2] + 8 + r,
                   kT2h[:, :, r * 128:(r + 1) * 128], 'kT2h', None)
        for r in range(16):
            v_tile(lambda kt, r=r: XTH[:, kt, r:HALO:16], 128, Wv_, wv_key, ['XTH'], V2h[:, r, :], 'V2h', None)
        flush_deferred(0)
        pre = {}
        S.barrier()
        if stop == 'halo':
            S.emit(st)
            return nc

        XTO = buf('XTO', [KT, TOK], BF16)
        XTS = buf('XTS', [KT, 4], BF16)
        XTH1 = buf('XTH1', [KT, 512], BF16)
        XTHc = buf('XTHc', [KT, 128], BF16)
        xbs = buf('xbs', [D], BF16)
        for t in range(NTILE):
            xi = nxt('xb', 2)
            build_xt(x_own[t * 128:(t + 1) * 128, :], 128, xbb[xi], f'xb{xi}',
                     lambda kt0, t=t: XTO[:, kt0:kt0 + 8, t * 128:(t + 1) * 128], 'XTO', f'xb{xi}')
        xi = nxt('xb', 2)
        build_xt(x_halo[1920:2048, :], 128, xbb[xi], f'xb{xi}', lambda kt0: XTHc[:, kt0:kt0 + 8, :], 'XTHc', f'xb{xi}')
        build_xt(x_s, 4, xbs, 'xbs', lambda kt0: XTS[:, kt0:kt0 + 8, :], 'XTS', 'xbs')
        XK = ['XTO', 'XTS', 'XTH1', 'XTHc']
        pre['a'] = load_w(wblk(4608), (KT, 512))
        pre['g'] = load_w(wblk(5120), (KT, 512))
        S.barrier()

        uT = buf('uT', [4, 1054], F32)
        uTs = buf('uTs', [4, 34], F32)
        uTb = buf('uTb', [4, 1054], BF16)
        uTsb = buf('uTsb', [4, 34], BF16)
        dgs = [buf(f'dg{i}', [31, 128], BF16) for i in range(4)]
        cT = buf('cT', [4, TS], F32)
        sq = buf('sq', [4, 512], F32)
        scv = buf('scv', [512], F32)
        cvo = buf('cvo', [512], F32)
        cvo2 = buf('cvo2', [512], F32)
        sgs = [buf(f'sg{i}', [512], F32) for i in range(2)]
        cmean = buf('cv_mean', [512], F32)
        cm2 = buf('cv_m2', [512], F32)
        crstd = buf('cv_rstd', [512], F32)
        cyt = buf('cv_yt', [512], F32)
        mixTc = buf('mixTc', [4, TS], BF16)
        nbs = [('o0', lambda kt: XTO[:, kt, 0:512], 512, lambda c: uT[:, c, 30:542], lambda c: uTb[:, c, 30:542]),
               ('o1', lambda kt: XTO[:, kt, 512:1024], 512, lambda c: uT[:, c, 542:1054], lambda c: uTb[:, c, 542:1054]),
               ('h', lambda kt: XTHc[:, kt, 98:128], 30, lambda c: uT[:, c, 0:30], lambda c: uTb[:, c, 0:30]),
               ('s', lambda kt: XTS[:, kt, 0:4], 4, lambda c: uTs[:, c, 30:34], lambda c: uTsb[:, c, 30:34])]
        Wa, wa_key = pre.pop('a')
        Wg, wg_key = pre.pop('g')
        sgc = [0]
        for which, (Wx, wx_key) in enumerate([(Wa, wa_key), (Wg, wg_key)]):
            for c in range(4):
                for (nm, rhs_fn, N, dst_fn, dstb_fn) in nbs:
                    ai = nxt('acc', 4)
                    acc = accs[ai]
                    for kt in range(KT):
                        S.op('pe', lambda e, kt=kt, acc=acc, rhs_fn=rhs_fn, N=N, Wx=Wx, c=c: e.matmul(acc[:, 0:N], lhsT=Wx[:, kt, c * 128:(c + 1) * 128], rhs=rhs_fn(kt), start=(kt == 0), stop=(kt == KT - 1)),
                             r=XK + [wx_key], w=[f'acc{ai}'])
                    dst = dst_fn(c)
                    ukey = f'u{c}{nm}'
                    if which == 0:
                        S.op('act', lambda e, dst=dst, acc=acc, N=N: e.copy(out=dst, in_=acc[:, 0:N]), r=[f'acc{ai}'], w=[ukey])
                    else:
                        si = sgc[0] % 2
                        sgc[0] += 1
                        sg = sgs[si]
                        S.op('act', lambda e, sg=sg, acc=acc, N=N: e.activation(out=sg[:, 0:N], in_=acc[:, 0:N], func=AF.Sigmoid), r=[f'acc{ai}'], w=[f'sg{si}'])
                        S.op('dve', lambda e, dst=dst, sg=sg, N=N: e.tensor_tensor(dst, dst, sg[:, 0:N], op=ALU.mult), r=[f'sg{si}', ukey], w=[ukey])
                        dstb = dstb_fn(c)
                        S.op('act', lambda e, dst=dst, dstb=dstb: e.copy(out=dstb, in_=dst), r=[ukey], w=[ukey + 'b'])
        for c in range(4):
            S.op('dve', lambda e, c=c: e.tensor_tensor(dgs[c], ident_f.unsqueeze(1).to_broadcast([128, 31, 128]), cw[:, c, :].unsqueeze(2).to_broadcast([128, 31, 128]), op=ALU.mult),
                 r=['ident_f', 'cw'], w=[f'dg{c}'])
        pre[('q', 2)] = load_w(wblk(512 * 2), (KT, 512))
        pre[('k', 2)] = load_w(wblk(1536 + 512 * 2), (KT, 512))
        S.dma('sp', 'scv', lambda e: e.dma_start(out=scv[0:30, :], in_=sconv), w=['scv'])
        S.dma('sp', 'd2d', lambda e: e.dma_start(out=convs_out[0:26, :], in_=sconv[4:30, :]))
        for c in range(4):
            S.op('pe', lambda e, c=c: e.transpose(pSs[0][:, c * 32:c * 32 + 30], scv[0:30, c * 128:(c + 1) * 128], ident_f[0:30, 0:30]), r=['scv', 'ident_f'], w=['pS0'])
        S.op('dve', lambda e: e.tensor_copy(uTs[:, :, 0:30], pSs[0][:, 0:128].rearrange("p (c j) -> p c j", c=4)[:, :, 0:30]), r=['pS0'], w=['uTs_h'])
        S.op('act', lambda e: e.copy(out=uTsb[:, :, 0:30], in_=uTs[:, :, 0:30]), r=['uTs_h'], w=['uTs_hb'])
        ukeys_all = [f'u{c}{nm}' for c in range(4) for nm in ('o0', 'o1', 'h', 's')] + ['uTs_h']
        for c in range(4):
            S.op('pe', lambda e, c=c: e.transpose(pSs[1][0:30, c * 128:(c + 1) * 128], uT[:, c, 1024:1054], ident_f[:, :]), r=ukeys_all + ['ident_f'], w=['pS1'])
        S.op('dve', lambda e: e.tensor_copy(cvo[0:30, :], pSs[1][0:30, :]), r=['pS1'], w=['cvo'])
        S.dma('sp', 'cvo', lambda e: e.dma_start(out=conv_out, in_=cvo[0:30, :]), r=['cvo'])
        for c in range(4):
            S.op('pe', lambda e, c=c: e.transpose(pSs[0][0:4, c * 128:(c + 1) * 128], uTs[:, c, 30:34], ident_f[:, :]), r=ukeys_all + ['ident_f'], w=['pS0'])
        S.op('dve', lambda e: e.tensor_copy(cvo2[0:4, :], pSs[0][0:4, :]), r=['pS0'], w=['cvo2'])
        S.dma('sp', 'cvo', lambda e: e.dma_start(out=convs_out[26:30, :], in_=cvo2[0:4, :]), r=['cvo2'])
        S.barrier()
        ubkeys = [f'u{c}{nm}b' for c in range(4) for nm in ('o0', 'o1', 'h', 's')] + ['uTs_hb']
        dgc = [0]
        for c in range(4):
            b0, b1, bs_ = (accs[0], accs[1], pSs[0]) if c % 2 == 0 else (accs[2], accs[3], pSs[1])
            k0, k1, ks_ = ('acc0', 'acc1', 'pS0') if c % 2 == 0 else ('acc2', 'acc3', 'pS1')
            di = c
            dgA = dgs[di]
            for j in range(31):
                dg = dgA[:, j, :]
                S.op('pe', lambda e, dg=dg, c=c, j=j, b0=b0: e.matmul(b0[:, 0:512], lhsT=dg, rhs=uTb[:, c, j:j + 512], start=(j == 0), stop=(j == 30)), r=ubkeys + [f'dg{di}'], w=[k0])
                S.op('pe', lambda e, dg=dg, c=c, j=j, b1=b1: e.matmul(b1[:, 0:512], lhsT=dg, rhs=uTb[:, c, 512 + j:1024 + j], start=(j == 0), stop=(j == 30)), r=ubkeys + [f'dg{di}'], w=[k1])
                S.op('pe', lambda e, dg=dg, c=c, j=j, bs_=bs_: e.matmul(bs_[:, 0:4], lhsT=dg, rhs=uTsb[:, c, j:j + 4], start=(j == 0), stop=(j == 30)), r=ubkeys + [f'dg{di}'], w=[ks_])
            S.op('act', lambda e, c=c, b0=b0: e.activation(out=cT[:, c, 0:512], in_=b0[:, 0:512], func=AF.Identity, bias=cv[:, c:c + 1]), r=[k0, 'cvec'], w=[f'cT{c}'])
            S.op('act', lambda e, c=c, b1=b1: e.activation(out=cT[:, c, 512:1024], in_=b1[:, 0:512], func=AF.Identity, bias=cv[:, c:c + 1]), r=[k1, 'cvec'], w=[f'cT{c}'])
            S.op('act', lambda e, c=c, bs_=bs_: e.activation(out=cT[:, c, TOK:TS], in_=bs_[:, 0:4], func=AF.Identity, bias=cv[:, c:c + 1]), r=[ks_, 'cvec'], w=[f'cTs{c}'])
        ckeys = [f'cT{c}' for c in range(4)] + [f'cTs{c}' for c in range(4)]
        sqkeys = [f'sq{c}' for c in range(4)]
        def f32view(name, n):
            off = A.off[name]
            return arena[:, off:off + n * 4].bitcast(F32)
        cmean_all = f32view('dg0', TS)
        crstd_all = f32view('dg1', TS)
        cyt2 = f32view('dg2', 512)
        blocks = ((0, 512), (512, 512), (1024, 4))
        for (c0, N) in blocks:
            for c in range(4):
                S.op('act', lambda e, c=c, c0=c0, N=N: e.activation(out=sq[:, c, 0:N], in_=cT[:, c, c0:c0 + N], func=AF.Square), r=ckeys, w=[f'sq{c}'])
            for (srcT, bank, kk, o0) in ((cT, 0, ckeys, c0), (sq, 1, sqkeys, 0)):
                for c in range(4):
                    S.op('pe', lambda e, srcT=srcT, bank=bank, c=c, o0=o0, N=N: e.matmul(pSs[bank][:, 0:N], lhsT=ones_f[:, :], rhs=srcT[:, c, o0:o0 + N], start=(c == 0), stop=(c == 3)),
                         r=kk + ['ones_f'], w=[f'pS{bank}'])
            cm_ = cmean_all[:, c0:c0 + N]
            cr_ = crstd_all[:, c0:c0 + N]
            S.op('dve', lambda e, N=N, cm_=cm_: e.tensor_scalar(cm_, pSs[0][:, 0:N], 1.0 / 512, None, op0=ALU.mult), r=['pS0', 'dg0'], w=['dg0'])
            S.op('dve', lambda e, N=N, cm_=cm_: e.tensor_tensor(cm2[:, 0:N], cm_, cm_, op=ALU.mult), r=['dg0'], w=['cm2'])
            S.op('dve', lambda e, N=N: e.scalar_tensor_tensor(cm2[:, 0:N], pSs[1][:, 0:N], 1.0 / 512, cm2[:, 0:N], op0=ALU.mult, op1=ALU.subtract), r=['pS1', 'cm2'], w=['cm2'])
            S.op('dve', lambda e, N=N: e.tensor_scalar(cm2[:, 0:N], cm2[:, 0:N], EPS, None, op0=ALU.add), r=['cm2'], w=['cm2'])
            S.op('act', lambda e, N=N: e.activation(out=cm2[:, 0:N], in_=cm2[:, 0:N], func=AF.Ln), r=['cm2'], w=['cm2'])
            S.op('act', lambda e, N=N, cr_=cr_: e.activation(out=cr_, in_=cm2[:, 0:N], func=AF.Exp, scale=-0.5), r=['cm2', 'dg1'], w=['dg1'])
        cnt_ = [0]
        for (c0, N) in blocks:
            cm_ = cmean_all[:, c0:c0 + N]
            cr_ = crstd_all[:, c0:c0 + N]
            for c in range(4):
                yb, yk = ((cyt, 'cyt'), (cyt2, 'dg2'))[cnt_[0] % 2]
                cnt_[0] += 1
                S.op('dve', lambda e, c=c, c0=c0, N=N, yb=yb, cm_=cm_: e.tensor_tensor(yb[:, 0:N], cT[:, c, c0:c0 + N], cm_, op=ALU.subtract), r=ckeys + ['dg0', yk], w=[yk])
                S.op('dve', lambda e, N=N, yb=yb, cr_=cr_: e.tensor_tensor(yb[:, 0:N], yb[:, 0:N], cr_, op=ALU.mult), r=[yk, 'dg1'], w=[yk])
                S.op('dve', lambda e, c=c, N=N, yb=yb: e.tensor_scalar(yb[:, 0:N], yb[:, 0:N], cv[:, 4 + c:5 + c], cv[:, 8 + c:9 + c], op0=ALU.mult, op1=ALU.add), r=[yk, 'cvec'], w=[yk])
                S.op('act', lambda e, c=c, c0=c0, N=N, yb=yb: e.activation(out=mixTc[:, c, c0:c0 + N], in_=yb[:, 0:N], func=AF.Silu), r=[yk], w=['mixTc'])
        S.barrier()
        if stop == 'conv':
            S.emit(st)
            return nc

        Zt = buf('Z', [4, TS], F32)
        OTs = {g: buf(f'OT{g}', [4, TS], F32) for g in range(3)}
        ETs = [buf(f'ET{i}', [384], BF16) for i in range(4)]
        kvcb = [buf(f'kvcb{i}', [2, 512], BF16) for i in range(4)]
        kTc = [buf(f'kTc{i}', [4, 128], BF16) for i in range(4)]
        actr = dict(it=0, et=0, kvc=0, slot=0)

        def attn_A(it):
            g, nq, qT_ap, chunks = it['g'], it['nq'], it['qT'], it['chunks']
            slot = actr['slot'] % 4
            actr['slot'] += 1
            it['slot'] = slot
            sbank = accs[slot]
            skey = f'acc{slot}'
            ET = ETs[slot]
            for ci, ch in enumerate(chunks):
                nk = ch['nk']
                col = ci * (128 if nq == 128 else 8)
                S.op('pe', lambda e, ch=ch, nk=nk, col=col: e.matmul(sbank[0:nk, col:col + nq], lhsT=ch['kT'], rhs=qT_ap, start=True, stop=False, skip_group_check=True),
                     r=ch['keys'] + [f'qT{g}'], w=[skey])
                S.op('pe', lambda e, ch=ch, nk=nk, col=col: e.matmul(sbank[0:nk, col:col + nq], lhsT=ident_b[0:nk, 0:nk], rhs=ch['mask'], start=False, stop=True, skip_group_check=True),
                     r=['masks', 'ident_b'], w=[skey])
            if nq == 128 and all(ch['nk'] == 128 for ch in chunks):
                ncol = len(chunks) * 128
                S.op('act', lambda e: e.activation(out=ET[:, 0:ncol], in_=sbank[:, 0:ncol], func=AF.Exp, scale=SCALE), r=[skey], w=[f'ET{slot}'])
            else:
                for ci, ch in enumerate(chunks):
                    nk = ch['nk']
                    col = ci * (128 if nq == 128 else 8)
                    S.op('act', lambda e, nk=nk, col=col: e.activation(out=ET[0:nk, col:col + nq], in_=sbank[0:nk, col:col + nq], func=AF.Exp, scale=SCALE), r=[skey], w=[f'ET{slot}'])

        def attn_B(it):
            g, nq, chunks, ot_dst, z_dst, first_group = it['g'], it['nq'], it['chunks'], it['ot'], it['z'], it['first']
            slot = it['slot']
            ET = ETs[slot]
            par = actr['it'] % 2
            actr['it'] += 1
            obank = pSs[par]
            okey = f'pS{par}'
            n = len(chunks)
            for ci, ch in enumerate(chunks):
                nk = ch['nk']
                col = ci * (128 if nq == 128 else 8)
                S.op('pe', lambda e, ch=ch, nk=nk, col=col, fo=(ci == 0): e.matmul(obank[:, 0:nq], lhsT=ch['V'], rhs=ET[0:nk, col:col + nq], start=fo, stop=False, skip_group_check=True),
                     r=ch['keys'] + [f'ET{slot}'], w=[okey])
                S.op('pe', lambda e, nk=nk, col=col, last=(ci == n - 1): e.matmul(obank[:, 128:128 + nq], lhsT=ones_b[0:nk, :], rhs=ET[0:nk, col:col + nq], start=False, stop=last, skip_group_check=True),
                     r=[f'ET{slot}', 'ones_b'], w=[okey])
            S.op('act', lambda e: e.copy(out=ot_dst, in_=obank[:, 0:nq]), r=[okey], w=[f'OT{g}', f'ord{par}'])
            if first_group:
                S.op('dve', lambda e: e.tensor_copy(z_dst, obank[:, 128:128 + nq]), r=[okey, f'ord{par}'], w=['Z'])
            else:
                S.op('dve', lambda e: e.tensor_tensor(z_dst, z_dst, obank[:, 128:128 + nq], op=ALU.add), r=[okey, 'Z', f'ord{par}'], w=['Z'])

        def run_attention(its, hook_at=None, hook=None):
            n = len(its)
            for i in range(min(2, n)):
                attn_A(its[i])
            for i in range(n):
                attn_B(its[i])
                if hook is not None and i == hook_at:
                    hook()
                if i + 2 < n:
                    attn_A(its[i + 2])

        def _v(ps_ap, like):
            if len(like.shape) == 3:
                return ps_ap.rearrange("p (a b) -> p a b", a=like.shape[1])
            return ps_ap

        def own_lhsT(g, ti):
            if g == 0:
                return lambda kt: XTO[:, kt, ti * 128:(ti + 1) * 128]
            if g == 1:
                r4, n = ti // 2, ti % 2
                return lambda kt: XTO[:, kt, 512 * n + r4:512 * n + r4 + 509:4]
            return lambda kt: XTO[:, kt, ti:ti + 1017:8]

        def own_rows(g, ti, dram, kvsel):
            if g == 0:
                return [(0, 128, dram[ti * 128:(ti + 1) * 128, kvsel, :])]
            if g == 1:
                r4, n = ti // 2, ti % 2
                return [(0, 128, dram[512 * n + r4:512 * n + r4 + 509:4, kvsel, :])]
            return [(0, 128, dram[ti:ti + 1017:8, kvsel, :])]

        def own_cols(g, ti, T3, h):
            if g == 0:
                return T3[:, h, ti * 128:(ti + 1) * 128]
            if g == 1:
                r4, n = ti // 2, ti % 2
                return T3[:, h, 512 * n + r4:512 * n + r4 + 509:4]
            return T3[:, h, ti:ti + 1017:8]

        dil = {0: 1, 1: 4, 2: 16}
        nhalo = {0: 1, 1: 4, 2: 16}
        for g in (2, 1, 0):
            qT = buf(f'qT{g}', [4, TS], BF16)
            kTo = buf(f'kT{g}o', [4, TS], BF16)
            Vo = buf(f'V{g}o', [9, 512], BF16)
            if g == 2:
                kTh, Vh = kT2h, V2h
            else:
                kTh = buf(f'kT{g}h', [4, nhalo[g] * 128], BF16)
                Vh = buf(f'V{g}h', [nhalo[g], 512], BF16)
            if g == 1:
                for t in range(4):
                    build_xt(x_halo[1536 + t * 128:1536 + (t + 1) * 128, :], 128, xbb[0], 'xb0',
                             lambda kt0, t=t: XTH1[:, kt0:kt0 + 8, t * 128:(t + 1) * 128], 'XTH1', 'xb0')
            Wq, wq_key = pre.pop(('q', g)) if ('q', g) in pre else load_w(wblk(512 * g), (KT, 512))
            Wk, wk_key = pre.pop(('k', g)) if ('k', g) in pre else load_w(wblk(1536 + 512 * g), (KT, 512))
            for ti in range(NTILE):
                q_tile(own_lhsT(g, ti), 128, Wq, wq_key, XK, TAB_G[g] + ti, qT[:, :, ti * 128:(ti + 1) * 128], f'qT{g}')
            if 'A' not in SKIP:
                q_tile(lambda kt: XTS[:, kt, 0:4], 4, Wq, wq_key, XK, TAB_S, qT[:, :, TOK:TS], f'qT{g}')
            Wv_, wv_key = load_w(wblk(3072 + 512 * g), (KT, 512))
            for ti in range(NTILE):
                k_tile(own_lhsT(g, ti), 128, Wk, wk_key, XK, TAB_G[g] + ti, kTo[:, :, ti * 128:(ti + 1) * 128], f'kT{g}o', None if ('B' in SKIP or 'K' in SKIP) else own_rows(g, ti, kvout[g], 0))
            nb_ = kvc_in[g].shape[0]
            if 'A' not in SKIP:
                k_tile(lambda kt: XTS[:, kt, 0:4], 4, Wk, wk_key, XK, TAB_S, kTo[:, :, TOK:TS], f'kT{g}o', [(0, 4, kvs_out[g][nb_ - 4:nb_, 0, :])])
            if g == 1:
                for r in range(4):
                    k_tile(lambda kt, r=r: XTH1[:, kt, r:512:4], 128, Wk, wk_key, XK, TAB_G[1] + 8 + r, kTh[:, :, r * 128:(r + 1) * 128], 'kT1h', None)
            if g == 0:
                k_tile(lambda kt: XTH1[:, kt, 384:512], 128, Wk, wk_key, XK, TAB_G[0] + 8, kTh[:, :, 0:128], 'kT0h', None)
            for ti in range(NTILE):
                v_tile(own_lhsT(g, ti), 128, Wv_, wv_key, XK, Vo[:, ti, :], f'V{g}o', None if ('B' in SKIP or 'V' in SKIP) else own_rows(g, ti, kvout[g], 1))
            if 'A' not in SKIP:
                v_tile(lambda kt: XTS[:, kt, 0:4], 4, Wv_, wv_key, XK, Vo[0:4, 8, :], f'V{g}o', [(0, 4, kvs_out[g][nb_ - 4:nb_, 1, :])])
            if g == 1:
                for r in range(4):
                    v_tile(lambda kt, r=r: XTH1[:, kt, r:512:4], 128, Wv_, wv_key, XK, Vh[:, r, :], 'V1h', None)
            if g == 0:
                v_tile(lambda kt: XTH1[:, kt, 384:512], 128, Wv_, wv_key, XK, Vh[:, 0, :], 'V0h', None)
            flush_deferred(0)
            for r0_ in ([] if 'C' in SKIP else range(0, nb_ - 4, 128)):
                r1_ = min(r0_ + 128, nb_ - 4)
                S.dma('sp', 'd2d', lambda e, g=g, r0_=r0_, r1_=r1_: e.dma_start(out=kvs_out[g][r0_:r1_], in_=kvc_in[g][r0_ + 4:r1_ + 4]))
            S.barrier()
            d = dil[g]
            ntile_c = 1 if g == 0 else 4
            for tc_ in range(ntile_c):
                src = kvc_in[g][:, :, :] if g == 0 else kvc_in[g][tc_:nb_:d, :, :]
                S.dma('pool', f'kvcb{tc_}', lambda e, tc_=tc_, src=src: e.dma_start(out=kvcb[tc_], in_=src), w=[f'kvcb{tc_}'])
            if g > 0:
                pre[('q', g - 1)] = load_w(wblk(512 * (g - 1)), (KT, 512))
                pre[('k', g - 1)] = load_w(wblk(1536 + 512 * (g - 1)), (KT, 512))
            if g == 0:
                Wo = []
                for cb in range(4):
                    src3 = w_out[:, cb * 512:(cb + 1) * 512].rearrange("(kc p) c -> p kc c", p=128)
                    if cb < 2:
                        Wo.append(load_w(src3, (KT, 512)))
                    else:
                        wv_ = buf(f'Wo{cb}', [KT, 512], BF16)
                        S.dma('pool', f'Wo{cb}', lambda e, wv_=wv_, src3=src3: e.dma_start(out=wv_, in_=src3), w=[f'Wo{cb}'])
                        Wo.append((wv_, f'Wo{cb}'))

            if stop == f'g{g}p':
                S.emit(st)
                return nc
            OT = OTs[g]
            KO = [f'kT{g}o', f'V{g}o']
            KH = [f'kT{g}h', f'V{g}h']
            its = []
            for ti in range(NTILE):
                for h in range(4):
                    qT_ap = qT[:, h, ti * 128:(ti + 1) * 128]
                    hs = slice(h * 128, (h + 1) * 128)
                    same = dict(nk=128, kT=kTo[:, h, ti * 128:(ti + 1) * 128], V=Vo[:, ti, hs], mask=masks[:, 3 if g == 2 else 1, :], keys=KO)
                    if g == 2:
                        ha = dict(nk=128, kT=kTh[:, h, ti * 128:(ti + 1) * 128], V=Vh[:, ti, hs], mask=masks[:, 4, :], keys=KH)
                        hb = dict(nk=128, kT=kTh[:, h, (ti + 8) * 128:(ti + 9) * 128], V=Vh[:, ti + 8, hs], mask=masks[:, 12, :], keys=KH)
                        chunks = [same, ha, hb]
                    else:
                        if g == 0:
                            first = (ti == 0)
                            hidx = 0
                        else:
                            first = (ti % 2 == 0)
                            hidx = ti // 2
                        pt = ti - 1
                        if first:
                            prev = dict(nk=128, kT=kTh[:, h, hidx * 128:(hidx + 1) * 128], V=Vh[:, hidx, hs], mask=masks[:, 2, :], keys=KH)
                        else:
                            prev = dict(nk=128, kT=kTo[:, h, pt * 128:(pt + 1) * 128], V=Vo[:, pt, hs], mask=masks[:, 0, :], keys=KO)
                        chunks = [same, prev]
                    its.append(dict(g=g, nq=128, qT=qT_ap, chunks=chunks, ot=own_cols(g, ti, OT, h), z=own_cols(g, ti, Zt, h), first=(g == 2)))
            its_prompt = its
            if stop == f'g{g}ap':
                S.emit(st)
                return nc
            def cache_transposes(g=g, ntile_c=ntile_c):
                for tc_ in range(ntile_c):
                    ci = tc_
                    pi = nxt('pT', 2)
                    for h in range(4):
                        S.op('pe', lambda e, h=h, ci=ci, pi=pi: e.transpose(pTs[pi][:, h * 128:(h + 1) * 128], kvcb[ci][:, 0, h * 128:(h + 1) * 128], ident_b[:, :]), r=[f'kvcb{ci}', 'ident_b'], w=[f'pT{pi}'])
                    S.op('dve', lambda e, ci=ci, pi=pi: e.tensor_copy(kTc[ci], pTs[pi][:, 0:512].rearrange("p (a b) -> p a b", a=4)), r=[f'pT{pi}'], w=[f'kTc{ci}'])
            its = list(its_prompt)
            for h in range(4):
                hs = slice(h * 128, (h + 1) * 128)
                chunks = [dict(nk=4, kT=kTo[:, h, TOK:TS], V=Vo[0:4, 8, hs], mask=masks[0:4, 6 if g == 0 else 11, 0:4], keys=KO)]
                for tc_ in range(ntile_c):
                    chunks.append(dict(nk=128, kT=kTc[tc_][:, h, :], V=kvcb[tc_][:, 1, hs], mask=masks[:, 5 if g == 0 else 7 + tc_, 0:4], keys=[f'kTc{tc_}', f'kvcb{tc_}']))
                its.append(dict(g=g, nq=4, qT=qT[:, h, TOK:TS], chunks=chunks, ot=OT[:, h, TOK:TS], z=Zt[:, h, TOK:TS], first=(g == 2)))
            run_attention(its, hook_at=20, hook=cache_transposes)
            S.barrier()
            if stop == f'g{g}':
                S.emit(st)
                return nc

        mixTa = buf('mixTa', [12, TS], BF16)
        for h in range(4):
            S.op('act', lambda e, h=h: e.activation(out=Zt[:, h, :], in_=Zt[:, h, :], func=AF.Ln), r=['Z'], w=[f'Zr{h}'])
            S.op('act', lambda e, h=h: e.activation(out=Zt[:, h, :], in_=Zt[:, h, :], func=AF.Exp, scale=-1.0), r=[f'Zr{h}'], w=[f'Zr{h}'])
            for g in range(3):
                eng = 'dve'
                S.op(eng, lambda e, g=g, h=h: e.tensor_tensor(mixTa[:, 4 * g + h, :], OTs[g][:, h, :], Zt[:, h, :], op=ALU.mult), r=[f'OT{g}', f'Zr{h}'], w=[f'mixTa{g}{h}'])
        S.barrier()
        if stop == 'fin':
            S.emit(st)
            return nc

        stat = buf('stat', [18, 8], F32)
        lnc = [0]

        def layernorm(xap, M, gt, bt, junk, xkey, jkey='junk', pre_sums=None, act_norm=False, beta_eng='pool', defer_apply=False):
            si = lnc[0]
            lnc[0] += 1
            st_ = stat[0:M, si, :]
            sk = f'stat{si}'
            if pre_sums is None:
                S.op('dve', lambda e: e.memset(st_[:, 0:2], 0.0), w=[sk])
                S.op('act', lambda e: e.activation(out=junk[0:M, :], in_=xap, func=AF.Identity, accum_out=st_[:, 0:1]), r=[xkey, sk], w=[sk, jkey])
                S.op('act', lambda e: e.activation(out=junk[0:M, :], in_=xap, func=AF.Square, accum_out=st_[:, 1:2]), r=[xkey, sk], w=[sk, jkey])
            else:
                a1, a2, pk = pre_sums
                S.op('dve', lambda e: e.reduce_sum(out=st_[:, 0:1], in_=a1, axis=mybir.AxisListType.X), r=[pk], w=[sk])
                S.op('dve', lambda e: e.reduce_sum(out=st_[:, 1:2], in_=a2, axis=mybir.AxisListType.X), r=[pk, sk], w=[sk])
            S.op('dve', lambda e: e.tensor_scalar(st_[:, 2:3], st_[:, 0:1], 1.0 / D, None, op0=ALU.mult), r=[sk], w=[sk])
            S.op('dve', lambda e: e.tensor_tensor(st_[:, 3:4], st_[:, 2:3], st_[:, 2:3], op=ALU.mult), r=[sk], w=[sk])
            S.op('dve', lambda e: e.scalar_tensor_tensor(st_[:, 3:4], st_[:, 1:2], 1.0 / D, st_[:, 3:4], op0=ALU.mult, op1=ALU.subtract), r=[sk], w=[sk])
            S.op('dve', lambda e: e.tensor_scalar(st_[:, 3:4], st_[:, 3:4], EPS, None, op0=ALU.add), r=[sk], w=[sk])
            S.op('act', lambda e: e.activation(out=st_[:, 3:4], in_=st_[:, 3:4], func=AF.Sqrt), r=[sk], w=[sk])
            S.op('dve', lambda e: e.reciprocal(st_[:, 3:4], st_[:, 3:4]), r=[sk], w=[sk])
            S.op('dve', lambda e: e.scalar_tensor_tensor(st_[:, 4:5], st_[:, 2:3], -1.0, st_[:, 3:4], op0=ALU.mult, op1=ALU.mult), r=[sk], w=[sk])

            def apply():
                if act_norm:
                    S.op('act', lambda e: e.activation(out=xap, in_=xap, func=AF.Identity, scale=st_[:, 3:4], bias=st_[:, 4:5]), r=[xkey, sk], w=[xkey])
                else:
                    S.op('dve', lambda e: e.tensor_scalar(xap, xap, st_[:, 3:4], st_[:, 4:5], op0=ALU.mult, op1=ALU.add), r=[xkey, sk], w=[xkey])
                S.op('dve', lambda e: e.tensor_tensor(xap, xap, gt[0:M, :], op=ALU.mult), r=[xkey, 'lnp'], w=[xkey])
                S.op(beta_eng, lambda e: e.tensor_tensor(xap, xap, bt[0:M, :], op=ALU.add), r=[xkey, 'lnp'], w=[xkey])
            if defer_apply:
                return apply
            apply()

        def tile_rows(ti):
            return (ti * 128, 128) if ti < 8 else (TOK, 4)

        x1ps = [buf(f'x1p{i}', [D], F32) for i in range(3)]
        x1T = buf('x1T', [KT, TS], BF16)
        lng = buf('lng', [D], F32)
        lnb = buf('lnb', [D], F32)
        xps = [buf(f'xp{i}', [D], F32) for i in range(2)]
        x1bs = [buf(f'x1b{i}', [D], BF16) for i in range(2)]
        S.dma('sp', 'lnp', lambda e: e.dma_start(out=lng, in_=lnp[0].partition_broadcast(128)), w=['lnp'])
        S.dma('sp', 'lnp', lambda e: e.dma_start(out=lnb, in_=lnp[1].partition_broadcast(128)), w=['lnp'])
        ctr['xp'] = 0
        ctr['x1b'] = 0
        ctr['x1p'] = 0

        def x1_transposes(bi, r0, M):
            for half in range(2):
                pi = nxt('pT', 2)
                for j in range(8):
                    kt = half * 8 + j
                    S.op('pe', lambda e, kt=kt, j=j, pi=pi: e.transpose(pTs[pi][:, j * 128:j * 128 + M], x1bs[bi][0:M, kt * 128:(kt + 1) * 128], ident_b[0:M, 0:M]),
                         r=[f'x1b{bi}', 'ident_b'], w=[f'pT{pi}'])
                dst = x1T[:, half * 8:half * 8 + 8, r0:r0 + M]
                src = pTs[pi][:, :].rearrange("p (a b) -> p a b", a=8)[:, :, 0:M]
                if half == 0:
                    S.op('act', lambda e, dst=dst, src=src: e.copy(out=dst, in_=src), r=[f'pT{pi}'], w=['x1T'])
                else:
                    S.op('dve', lambda e, dst=dst, src=src: e.tensor_copy(dst, src), r=[f'pT{pi}'], w=['x1T'])

        def ffn_load(s_):
            slot = nxt('w', 2)
            Wv_ = Wb[slot][:, :].rearrange("p (g k c) -> p g k c", g=2, k=KT)
            wkey_ = f'W{slot}'
            for gi, wsrc in enumerate((w_gate, w_up)):
                S.dma('pool', wkey_, lambda e, Wv_=Wv_, gi=gi, wsrc=wsrc: e.dma_start(out=Wv_[:, gi, :, :], in_=wsrc[:, s_ * 256:(s_ + 1) * 256].rearrange("(kt p) c -> p kt c", p=128)), w=[wkey_])
            return Wv_, wkey_
        ffn_pre = {}
        for ti in range(9):
            r0, M = tile_rows(ti)
            xi = nxt('xp', 2)
            pi_ = nxt('x1p', 3)
            xk = f'x1p{pi_}'
            src = x_own[r0:r0 + M, :] if ti < 8 else x_s
            S.dma('sp', f'xp{xi}', lambda e, xi=xi, M=M, src=src: e.dma_start(out=xps[xi][0:M, :], in_=src), w=[f'xp{xi}'])
            lf = (lambda r0, M: (lambda kc: mixTa[:, kc, r0:r0 + M] if kc < 12 else mixTc[:, kc - 12, r0:r0 + M]))(r0, M)
            for cb in range(4):
                acc, akey = proj_tm(lf, M, Wo[cb][0], Wo[cb][1], ['mixTa', 'mixTc'])
                S.op('dve', lambda e, xi=xi, M=M, pi_=pi_, cb=cb, acc=acc: e.scalar_tensor_tensor(x1ps[pi_][0:M, cb * 512:(cb + 1) * 512], xps[xi][0:M, cb * 512:(cb + 1) * 512], ALPHA, acc[0:M, :], op0=ALU.mult, op1=ALU.add),
                     r=[akey, f'xp{xi}'], w=[xk])
            if ti == 8:
                ffn_pre[0] = ffn_load(0)
                ffn_pre[1] = ffn_load(1)
            bi = nxt('x1b', 2)
            flush_deferred(0)
            layernorm(x1ps[pi_][0:M, :], M, lng, lnb, x1bs[bi], xk, jkey=f'x1b{bi}')

            def stage2(bi=bi, r0=r0, M=M, pi_=pi_, xk=xk, ti=ti):
                S.dma('pool', f'x1st{pi_}', lambda e: e.dma_start(out=x1s[r0:r0 + M, :], in_=x1ps[pi_][0:M, :]), r=[xk], w=[f'x1s{ti}'])
                S.op('act', lambda e: e.copy(out=x1bs[bi][0:M, :], in_=x1ps[pi_][0:M, :]), r=[xk], w=[f'x1b{bi}'])
                x1_transposes(bi, r0, M)
            deferred.append(stage2)
        flush_deferred(0)

        S.barrier()
        if stop == 'wout':
            S.emit(st)
            return nc

        fT = buf('fT', [NFC, 1152], BF16)
        sgf = [buf(f'sgf{i}', [512], F32) for i in range(2)]
        ctr['sgf'] = 0
        for s in range(22):
            Wv, wkey = ffn_pre.pop(s) if s in ffn_pre else ffn_load(s)
            for ch in range(2):
                fc = 2 * s + ch
                for (c0, N) in ((0, 512), (512, 512), (TOK, 4)):
                    ag = nxt('acc', 4)
                    au = nxt('acc', 4)
                    for (gi, ai) in ((0, ag), (1, au)):
                        for kt in range(KT):
                            S.op('pe', lambda e, gi=gi, ai=ai, kt=kt, Wv=Wv, ch=ch, c0=c0, N=N: e.matmul(accs[ai][:, 0:N], lhsT=Wv[:, gi, kt, ch * 128:(ch + 1) * 128], rhs=x1T[:, kt, c0:c0 + N], start=(kt == 0), stop=(kt == KT - 1)),
                                 r=['x1T', wkey], w=[f'acc{ai}'])
                    si = nxt('sgf', 2)
                    S.op('act', lambda e, si=si, ag=ag, N=N: e.activation(out=sgf[si][:, 0:N], in_=accs[ag][:, 0:N], func=AF.Silu), r=[f'acc{ag}'], w=[f'sgf{si}'])
                    S.op('dve', lambda e, si=si, au=au, N=N, fc=fc, c0=c0: e.tensor_tensor(fT[:, fc, c0:c0 + N], sgf[si][:, 0:N], accs[au][:, 0:N], op=ALU.mult), r=[f'sgf{si}', f'acc{au}'], w=['fT'])
        S.op('dve', lambda e: e.memset(fT[:, :, TS:1152], 0.0), w=['fTpad'])
        def wdn_src(cb_, half_):
            return w_down[half_ * 2816:(half_ + 1) * 2816, cb_ * 256:(cb_ + 1) * 256].rearrange("(fc p) c -> p fc c", p=128)
        wdn_pre = {(0, 0): load_w(wdn_src(0, 0), (22, 256)), (0, 1): load_w(wdn_src(0, 1), (22, 256))}
        S.barrier()
        if stop == 'ffn':
            S.emit(st)
            return nc

        dxb = [buf(f'dxb{i}', [9, 256], F32) for i in range(2)]
        lng2 = buf('lng2', [D], F32)
        lnb2 = buf('lnb2', [D], F32)
        S.dma('sp', 'lnp2', lambda e: e.dma_start(out=lng2, in_=lnp[2].partition_broadcast(128)), w=['lnp'])
        S.dma('sp', 'lnp2', lambda e: e.dma_start(out=lnb2, in_=lnp[3].partition_broadcast(128)), w=['lnp'])
        for ti in range(9):
            r0, M = tile_rows(ti)
            S.dma('sp', f'ybeta{ti}', lambda e, r0=r0, M=M: e.dma_start(out=y_out[r0:r0 + M, :], in_=lnb2[0:M, :]), r=['lnp'], w=[f'yout{ti}'])
        ls1 = buf('ls1', [9, 8], F32)
        ls2 = buf('ls2', [9, 8], F32)
        ljunk = buf('ljunk', [256], BF16)
        ljunk2 = buf('ljunk2', [256], BF16)
        S.op('dve', lambda e: e.memset(ls1, 0.0), w=['lsa'])
        S.op('dve', lambda e: e.memset(ls2, 0.0), w=['ls'])
        dyb = [buf(f'dyb{i}', [9, 256], F32) for i in range(2)]

        def dbank(ti):
            return (accs[ti // 2], (ti % 2) * 256) if ti < 8 else (pSs[0], 0)

        x1keys = [f'x1s{ti}' for ti in range(9)]
        for cb in range(8):
            bi = cb % 2
            cs = slice(cb * 256, (cb + 1) * 256)
            S.dma('sp', f'dxb{bi}', lambda e, bi=bi, cs=cs: e.dma_start(out=dxb[bi][:, 0:8, :], in_=x1s[0:TOK, cs].rearrange("(t p) c -> p t c", p=128)), r=x1keys, w=[f'dxb{bi}'])
            S.dma('sp', f'dxb{bi}', lambda e, bi=bi, cs=cs: e.dma_start(out=dxb[bi][0:4, 8, :], in_=x1s[TOK:TS, cs]), r=x1keys, w=[f'dxb{bi}'])
            for half in range(2):
                Wv, wkey = wdn_pre.pop((cb, half)) if (cb, half) in wdn_pre else load_w(wdn_src(cb, half), (22, 256))
                for j in range(22):
                    fc = half * 22 + j
                    for ti in range(9):
                        r0, M = tile_rows(ti)
                        bank, colo = dbank(ti)
                        first = (fc == 0) and (ti % 2 == 0)
                        S.op('pe', lambda e, bank=bank, colo=colo, fc=fc, r0=r0, Wv=Wv, j=j, first=first: e.matmul(bank[:, colo:colo + 256], lhsT=fT[:, fc, r0:r0 + 128], rhs=Wv[:, j, :], start=first, stop=(fc == NFC - 1), skip_group_check=True),
                             r=['fT', wkey], w=[f'dacc{ti // 2}'])
            for ti in range(9):
                r0, M = tile_rows(ti)
                bank, colo = dbank(ti)
                S.op('dve', lambda e, bi=bi, ti=ti, M=M, bank=bank, colo=colo: e.scalar_tensor_tensor(dyb[bi][0:M, ti, :], dxb[bi][0:M, ti, :], ALPHA, bank[0:M, colo:colo + 256], op0=ALU.mult, op1=ALU.add),
                     r=[f'dacc{ti // 2}', f'dxb{bi}'], w=[f'dyb{bi}_{ti}'])
            for ti in range(9):
                r0, M = tile_rows(ti)
                S.op('dve', lambda e, bi=bi, ti=ti, M=M, cb=cb: e.tensor_scalar(ljunk2[0:M, :], dyb[bi][0:M, ti, :], 1.0, 0.0, op0=ALU.mult, op1=ALU.add, accum_out=ls1[0:M, ti, cb:cb + 1]), r=[f'dyb{bi}_{ti}', 'lsa'], w=[f'lsa{ti}_{cb}', 'ljunk2'])
                S.op('act', lambda e, bi=bi, ti=ti, M=M, cb=cb: e.activation(out=ljunk[0:M, :], in_=dyb[bi][0:M, ti, :], func=AF.Square, accum_out=ls2[0:M, ti, cb:cb + 1]), r=[f'dyb{bi}_{ti}', 'ls'], w=[f'ls{ti}_{cb}', 'ljunk'])
            S.dma('sp', f'dyb{bi}', lambda e, bi=bi, cs=cs: e.dma_start(out=y2s[0:TOK, cs].rearrange("(t p) c -> p t c", p=128), in_=dyb[bi][:, 0:8, :]), r=[f'dyb{bi}_{t_}' for t_ in range(9)], w=[f'y2s{cb}'])
            S.dma('sp', f'dyb{bi}', lambda e, bi=bi, cs=cs: e.dma_start(out=y2s[TOK:TS, cs], in_=dyb[bi][0:4, 8, :]), r=[f'dyb{bi}_{t_}' for t_ in range(9)], w=[f'y2s{cb}'])
        S.barrier()
        if stop == 'wdn':
            S.emit(st)
            return nc

        yts = [buf(f'yt{i}', [D], F32) for i in range(5)]
        lnjunk2 = buf('lnjunk2', [D], F32)
        ctr['yt'] = 0
        st9 = buf('stat', [18, 8], F32)[:, 0:8, :].rearrange("p a b -> p (a b)")
        sA, sB, mean9, var9, rstd9, nb9 = (st9[:, 9 * i:9 * i + 9] for i in range(6))
        S.op('dve', lambda e: e.reduce_sum(out=sA, in_=ls1, axis=mybir.AxisListType.X), w=['st9'])
        S.op('dve', lambda e: e.reduce_sum(out=sB, in_=ls2, axis=mybir.AxisListType.X), r=['st9'], w=['st9'])
        S.op('dve', lambda e: e.tensor_scalar(mean9, sA, 1.0 / D, None, op0=ALU.mult), r=['st9'], w=['st9'])
        S.op('dve', lambda e: e.tensor_tensor(var9, mean9, mean9, op=ALU.mult), r=['st9'], w=['st9'])
        S.op('dve', lambda e: e.scalar_tensor_tensor(var9, sB, 1.0 / D, var9, op0=ALU.mult, op1=ALU.subtract), r=['st9'], w=['st9'])
        S.op('dve', lambda e: e.tensor_scalar(var9, var9, EPS, None, op0=ALU.add), r=['st9'], w=['st9'])
        S.op('act', lambda e: e.activation(out=var9, in_=var9, func=AF.Ln), r=['st9'], w=['st9'])
        S.op('act', lambda e: e.activation(out=rstd9, in_=var9, func=AF.Exp, scale=-0.5), r=['st9'], w=['st9'])
        S.op('dve', lambda e: e.scalar_tensor_tensor(nb9, mean9, -1.0, rstd9, op0=ALU.mult, op1=ALU.mult), r=['st9'], w=['st9'])
        for ti in range(9):
            r0, M = tile_rows(ti)
            yi = nxt('yt', 5)
            xap = yts[yi][0:M, :]
            S.dma('sp', f'yt{yi}', lambda e, yi=yi, r0=r0, M=M: e.dma_start(out=yts[yi][0:M, :], in_=y2s[r0:r0 + M, :]), r=[f'y2s{cb}' for cb in range(8)], w=[f'yt{yi}'])
            S.op('act', lambda e, xap=xap, ti=ti, M=M: e.activation(out=xap, in_=xap, func=AF.Identity, scale=rstd9[0:M, ti:ti + 1], bias=nb9[0:M, ti:ti + 1]), r=[f'yt{yi}', 'st9'], w=[f'yt{yi}'])
            S.op('dve', lambda e, xap=xap, M=M: e.tensor_tensor(xap, xap, lng2[0:M, :], op=ALU.mult), r=[f'yt{yi}', 'lnp'], w=[f'yt{yi}'])
            S.dma('pool', f'yto{yi}', lambda e, yi=yi, r0=r0, M=M: e.dma_start(out=y_out[r0:r0 + M, :], in_=yts[yi][0:M, :], accum_op=ALU.add), r=[f'yt{yi}', f'yout{ti}'])
        S.emit(st)
    return nc


def _rope_tables(T0):
    half = 64
    inv = (10000.0 ** (-np.arange(half, dtype=np.float32) / np.float32(half))).astype(np.float32)
    p = np.arange(128)
    pos = np.zeros((NTAB, 128), dtype=np.int64)
    for n in range(8):
        pos[TAB_G[0] + n] = T0 + 128 * n + p
    pos[TAB_G[0] + 8] = T0 - 128 + p
    for r4 in range(4):
        for n in range(2):
            pos[TAB_G[1] + r4 * 2 + n] = T0 + r4 + 4 * (128 * n + p)
        pos[TAB_G[1] + 8 + r4] = T0 - 512 + r4 + 4 * p
    for t in range(8):
        pos[TAB_G[2] + t] = T0 + t + 8 * p
    for r in range(16):
        pos[TAB_G[2] + 8 + r] = T0 - 2048 + r + 16 * p
    pos[TAB_S] = PAST + np.minimum(p, 3)
    pos = np.maximum(pos, 0)
    ang = pos.astype(np.float32)[:, :, None] * inv[None, None, :]
    cos = np.cos(ang).astype(np.float32)
    sin = np.sin(ang).astype(np.float32)
    tab = np.concatenate([cos, cos, -sin, sin], axis=-1).astype(np.float32)
    return np.ascontiguousarray(tab)


def _masks(j):
    m = np.full((128, NMASK, 128), NEG, dtype=np.float32)
    b = np.arange(128)[:, None]
    a = np.arange(128)[None, :]
    m[:, 0, :] = np.where(b >= a, 0.0, NEG)
    m[:, 1, :] = np.where(b <= a, 0.0, NEG)
    m[:, 2, :] = m[:, 0, :] if j > 0 else NEG
    m[:, 3, :] = np.where(((b % 2) == (a % 2)) & ((b // 2) <= (a // 2)), 0.0, NEG)
    valid = b >= (a // 2)
    if j == 0:
        valid = valid & False
    elif j == 1:
        valid = valid & (b >= 64)
    m[:, 4, :] = np.where(valid & (a % 2 == 0), 0.0, NEG)
    m[:, 12, :] = np.where(valid & (a % 2 == 1), 0.0, NEG)
    m[:, 5, :] = np.where(b >= a, 0.0, NEG)
    m[:, 6, :] = np.where(b <= a, 0.0, NEG)
    for t in range(4):
        m[:, 7 + t, :] = np.where(a == t, 0.0, NEG) + 0.0 * b
    m[:, 11, :] = np.where(b == a, 0.0, NEG)
    return np.ascontiguousarray(m)


_NC_CACHE = {}


def kernel(x_prompt, x_sample, cache_kv_w128, cache_kv_w512, cache_kv_w2048, state_conv,
           w_in, w_out, conv_w, conv_b, conv_ln_g, conv_ln_b, ln1_g, ln1_b,
           w_gate, w_up, w_down, ln2_g, ln2_b):
    f32 = np.float32
    x_prompt = np.asarray(x_prompt, f32)
    x_sample = np.asarray(x_sample, f32)
    caches = [np.asarray(c, f32) for c in (cache_kv_w128, cache_kv_w512, cache_kv_w2048)]
    state_conv = np.asarray(state_conv, f32)
    if 'nc' not in _NC_CACHE:
        _NC_CACHE['nc'] = build_program(_NC_CACHE.get('stop'))
    nc = _NC_CACHE['nc']
    shared = {
        "w_in": np.ascontiguousarray(np.asarray(w_in, f32)[0]),
        "w_out": np.ascontiguousarray(np.asarray(w_out, f32)[0]),
        "w_gate": np.ascontiguousarray(np.asarray(w_gate, f32)[0]),
        "w_up": np.ascontiguousarray(np.asarray(w_up, f32)[0]),
        "w_down": np.ascontiguousarray(np.asarray(w_down, f32)[0]),
        "conv_wT": np.ascontiguousarray(np.asarray(conv_w, f32)[0].T),
        "cvec": np.ascontiguousarray(np.concatenate([np.asarray(v, f32)[0].reshape(4, 128).T for v in (conv_b, conv_ln_g, conv_ln_b)], axis=1)),
        "lnp": np.ascontiguousarray(np.stack([np.asarray(v, f32)[0] for v in (ln1_g, ln1_b, ln2_g, ln2_b)], axis=0)),
        "ident": np.eye(128, dtype=f32),
    }
    in_maps = []
    for c in range(NCORES):
        b, j = c // 4, c % 4
        T0 = 1024 * j
        xh = np.zeros((HALO, D), dtype=f32)
        lo = max(0, T0 - HALO)
        if T0 > 0:
            xh[HALO - (T0 - lo):] = x_prompt[b, lo:T0]
        m = dict(shared)
        m["x_own"] = np.ascontiguousarray(x_prompt[b, T0:T0 + TOK])
        m["x_halo"] = xh
        m["x_s"] = np.ascontiguousarray(x_sample[c])
        for g in range(3):
            m[f"kvc{g}"] = np.ascontiguousarray(caches[g][0, c].reshape(-1, 2, 512))
        m["sconv"] = np.ascontiguousarray(state_conv[0, c])
        m["masks"] = _masks(j)
        m["rope"] = _rope_tables(T0)
        in_maps.append(m)
    res = run_bass_kernel_spmd(nc, in_maps, core_ids=list(range(NCORES)))
    R = res.results
    y_prompt = np.zeros((2, 4096, D), f32)
    y_sample = np.zeros((8, 4, D), f32)
    for c in range(NCORES):
        b, j = c // 4, c % 4
        y_prompt[b, 1024 * j:1024 * (j + 1)] = R[c]["y"][0:TOK]
        y_sample[c] = R[c]["y"][TOK:TS]
    kvp = []
    for g, keep in enumerate((128, 512, 2048)):
        out = np.zeros((1, 2, keep, 2, 4, 128), f32)
        for b in range(2):
            if keep <= 1024:
                out[0, b] = R[4 * b + 3][f"kvout{g}"][TOK - keep:TOK].reshape(keep, 2, 4, 128)
            else:
                out[0, b, 0:1024] = R[4 * b + 2][f"kvout{g}"].reshape(TOK, 2, 4, 128)
                out[0, b, 1024:2048] = R[4 * b + 3][f"kvout{g}"].reshape(TOK, 2, 4, 128)
        kvp.append(out)
    convp = np.stack([R[4 * b + 3]["convout"] for b in range(2)], axis=0)[None].astype(f32)
    kvs = []
    for g, nb in enumerate((128, 512, 2048)):
        kvs.append(np.stack([R[c][f"kvs{g}"].reshape(nb, 2, 4, 128) for c in range(NCORES)], axis=0)[None].astype(f32))
    convs = np.stack([R[c]["convs"] for c in range(NCORES)], axis=0)[None].astype(f32)
    return (y_prompt, y_sample, kvp[0], kvp[1], kvp[2], convp, kvs[0], kvs[1], kvs[2], convs)
```

```python
import numpy as np
from contextlib import ExitStack
import concourse.bass as bass
import concourse.mybir as mybir
from concourse.bass_utils import run_bass_kernel_spmd

F32 = mybir.dt.float32
BF16 = mybir.dt.bfloat16
U8 = mybir.dt.uint8
AF = mybir.ActivationFunctionType
ALU = mybir.AluOpType

D = 2048
KT = 16
TOK = 1024
TS = 1028
NTILE = 8
FF = 5632
NFC = 44
HALO = 2048
PAST = 16384
ALPHA = 2.0 ** 0.25
EPS = 1e-5
SCALE = 128.0 ** -0.5
NEG = -30000.0
NCORES = 8
TAB_G = {0: 0, 1: 9, 2: 21}
TAB_S = 45
NTAB = 46
NMASK = 13


class Sched:
    def __init__(self, nc):
        self.nc = nc
        self.ops = []

    def op(self, eng, fn, r=(), w=()):
        self.ops.append(dict(eng=eng, fn=fn, r=tuple(r), w=tuple(w), dma=None, bar=False))

    def dma(self, q, stream, fn, r=(), w=()):
        self.ops.append(dict(eng=q, fn=fn, r=tuple(r), w=tuple(w), dma=stream, bar=False))

    EXCL = ('W0', 'W1', 'Wo2', 'Wo3', 'd2d')

    def barrier(self):
        for e in ('pe', 'act', 'dve', 'pool', 'sp'):
            self.ops.append(dict(eng=e, fn=None, r=(), w=(), dma=None, bar=True))

    def emit(self, stack):
        nc = self.nc
        ops = self.ops
        n = len(ops)
        last_w = {}
        readers = {}
        deps = [None] * n
        last_eng = {}
        last_stream = {}
        i = 0
        while i < n:
            o = ops[i]
            if o['bar']:
                d = set(last_eng.values()) | set(v for k, v in last_stream.items() if k not in self.EXCL)
                j = i
                while j < n and ops[j]['bar']:
                    deps[j] = set(d)
                    j += 1
                last_w = {k: v for k, v in last_w.items() if ops[v]['dma'] in self.EXCL}
                readers = {}
                i = j
                continue
            d = set()
            for k in o['r']:
                if k in last_w:
                    d.add(last_w[k])
            for k in o['w']:
                if k in last_w:
                    d.add(last_w[k])
                for j in readers.get(k, ()):
                    d.add(j)
            d.discard(i)
            deps[i] = d
            for k in o['w']:
                last_w[k] = i
                readers[k] = []
            for k in o['r']:
                readers.setdefault(k, []).append(i)
            if o['dma'] is not None:
                last_stream[o['dma']] = i
            else:
                last_eng[o['eng']] = i
            i += 1
        signal = [False] * n
        for i, o in enumerate(ops):
            keep = set()
            for j in deps[i]:
                pj = ops[j]
                if (not o['bar']) and pj['dma'] is None and pj['eng'] == 'pe' and o['eng'] == 'pe' and o['dma'] is None:
                    continue
                if o['bar'] and pj['dma'] is None and pj['eng'] == o['eng'] and o['eng'] == 'pe':
                    continue
                keep.add(j)
                signal[j] = True
            deps[i] = keep
        cnt = {}
        semname = [None] * n
        semval = [0] * n
        for i, o in enumerate(ops):
            if o['bar']:
                continue
            if o['dma'] is not None:
                s = 'dma_' + o['dma']
                cnt[s] = cnt.get(s, 0) + 16
                semname[i] = s
                semval[i] = cnt[s]
            elif signal[i]:
                s = 'eng_' + o['eng']
                cnt[s] = cnt.get(s, 0) + 1
                semname[i] = s
                semval[i] = cnt[s]
        sems = {}
        for s in cnt:
            sems[s] = stack.enter_context(nc.semaphore(s))
        block = stack.enter_context(nc.Block())
        engs = {'pe': block.tensor, 'act': block.scalar, 'dve': block.vector,
                'pool': block.gpsimd, 'sp': block.sync}

        def make(engname):
            def body(e):
                waited = {}
                for i, o in enumerate(ops):
                    if o['eng'] != engname:
                        continue
                    need = {}
                    for j in deps[i]:
                        s = semname[j]
                        v = semval[j]
                        if waited.get(s, 0) >= v:
                            continue
                        need[s] = max(need.get(s, 0), v)
                    for s, v in need.items():
                        e.wait_ge(sems[s], v)
                        waited[s] = v
                    if o['fn'] is None:
                        continue
                    ins = o['fn'](e)
                    if semname[i] is not None:
                        ins.then_inc(sems[semname[i]], 16 if o['dma'] is not None else 1)
                if engname == 'sp':
                    for s, v in cnt.items():
                        if waited.get(s, 0) < v:
                            e.wait_ge(sems[s], v)
            return body
        for name, dec in engs.items():
            dec(make(name))


class Arena:
    def __init__(self):
        self.bufs = []

    def add(self, name, nbytes, first, last):
        self.bufs.append([name, (nbytes + 63) // 64 * 64, first, last, None])

    def solve(self):
        import random
        best = None
        keys = [lambda b: (-b[1],), lambda b: (-(b[3] - b[2]), -b[1]), lambda b: (b[2], -b[1]), lambda b: (-(b[1] * (b[3] - b[2] + 1)),),
                lambda b: (-(b[3] - b[2] + 1) * 4096 - b[1],)]
        orders = [sorted(self.bufs, key=k) for k in keys]
        rnd = random.Random(0)
        for _ in range(300):
            o = sorted(self.bufs, key=lambda b: (-(b[1] * (b[3] - b[2] + 1)) * (0.5 + rnd.random()),))
            orders.append(o)
        for order in orders:
            placed = []
            offs = {}
            for b in order:
                conflicts = sorted([(offs[p[0]], offs[p[0]] + p[1]) for p in placed if not (p[3] < b[2] or b[3] < p[2])])
                off = 0
                for s_, e_ in conflicts:
                    if off + b[1] <= s_:
                        break
                    off = max(off, e_)
                offs[b[0]] = off
                placed.append(b)
            size = max(offs[b[0]] + b[1] for b in self.bufs)
            if best is None or size < best[0]:
                best = (size, dict(offs))
        self.size, self.off = best
        return self.size


SKIP = set()


def build_program(stop=None):
    nc = bass.Bass("TRN2", target_bir_lowering=False)

    def din(name, shape, dt=F32):
        return nc.dram_tensor(name, list(shape), dt, kind="ExternalInput").ap()

    def dout(name, shape, dt=F32):
        return nc.dram_tensor(name, list(shape), dt, kind="ExternalOutput").ap()

    x_own = din("x_own", [TOK, D])
    x_halo = din("x_halo", [HALO, D])
    x_s = din("x_s", [4, D])
    kvc_in = [din("kvc0", [128, 2, 512]), din("kvc1", [512, 2, 512]), din("kvc2", [2048, 2, 512])]
    sconv = din("sconv", [30, 512])
    w_in = din("w_in", [D, FF])
    w_out = din("w_out", [D, D])
    w_gate = din("w_gate", [D, FF])
    w_up = din("w_up", [D, FF])
    w_down = din("w_down", [FF, D])
    conv_wT = din("conv_wT", [512, 31])
    cvec = din("cvec", [128, 12])
    lnp = din("lnp", [4, D])
    ident_d = din("ident", [128, 128])
    masks_d = din("masks", [128, NMASK, 128])
    rope_d = din("rope", [NTAB, 128, 256])

    y_out = dout("y", [TS, D])
    kvout = [dout(f"kvout{g}", [TOK, 2, 512]) for g in range(3)]
    kvs_out = [dout("kvs0", [128, 2, 512]), dout("kvs1", [512, 2, 512]), dout("kvs2", [2048, 2, 512])]
    conv_out = dout("convout", [30, 512])
    convs_out = dout("convs", [30, 512])
    x1s = nc.dram_tensor("x1s", [TS, D], F32).ap()
    y2s = nc.dram_tensor("y2s", [TS, D], F32).ap()

    PH = dict(init=0, halo=1, xt=2, conv=3, conv2=3.5, g2p=4, g2a=5, g1p=6, g1a=7, g0p=8, g0a=9, fin=10, wout=11, ffn=12, wdn=13, ln2=14)
    LASTPH = 14
    A = Arena()
    A.add('ident_f', 512, 0, LASTPH)
    A.add('ident_b', 256, 0, LASTPH)
    A.add('ones_b', 256, 0, LASTPH)
    A.add('ones_f', 512, 0, LASTPH)
    A.add('masks', NMASK * 128 * 2, 0, LASTPH)
    A.add('cw', 4 * 31 * 4, 0, LASTPH)
    A.add('cvec', 12 * 4, 0, LASTPH)
    A.add('W0', 16384, 0, LASTPH)
    A.add('W1', 16384, 0, LASTPH)
    A.add('XTH', KT * HALO * 2, PH['halo'], PH['halo'])
    A.add('kT2h', 4 * 2048 * 2, PH['halo'], PH['g2a'])
    A.add('V2h', 16 * 512 * 2, PH['halo'], PH['g2a'])
    A.add('xb0', D * 2, PH['halo'], PH['g1p'])
    A.add('xb1', D * 2, PH['halo'], PH['xt'])
    for i in range(2):
        A.add(f'tab{i}', 256 * 4, PH['halo'], PH['g0a'])
        A.add(f'kf{i}', 512 * 4, PH['halo'], PH['g0a'])
        A.add(f'qb{i}', 512 * 2, PH['halo'], PH['g0a'])
    A.add('qb2', 512 * 2, PH['halo'], PH['g0a'])
    A.add('tA', 512 * 4, PH['halo'], PH['g0a'])
    A.add('tB', 512 * 4, PH['halo'], PH['g0a'])
    A.add('XTO', KT * TOK * 2, PH['xt'], PH['g0p'])
    A.add('XTS', KT * 4 * 2, PH['xt'], PH['g0p'])
    A.add('XTH1', KT * 512 * 2, PH['g1p'], PH['g0p'])
    A.add('XTHc', KT * 128 * 2, PH['xt'], PH['conv'])
    A.add('xbs', D * 2, PH['xt'], PH['xt'])
    A.add('uT', 4 * 1054 * 4, PH['conv'], PH['conv'])
    A.add('uTs', 4 * 34 * 4, PH['conv'], PH['conv'])
    A.add('uTb', 4 * 1054 * 2, PH['conv'], PH['conv2'])
    A.add('uTsb', 4 * 34 * 2, PH['conv'], PH['conv2'])
    A.add('cT', 4 * TS * 4, PH['conv2'], PH['conv2'])
    A.add('sq', 4 * 512 * 4, PH['conv2'], PH['conv2'])
    A.add('scv', 512 * 4, PH['conv'], PH['conv'])
    A.add('cvo', 512 * 4, PH['conv'], PH['conv'])
    A.add('cvo2', 512 * 4, PH['conv'], PH['conv'])
    for i in range(2):
        A.add(f'sg{i}', 512 * 4, PH['conv'], PH['conv'])
    for i in range(4):
        A.add(f'dg{i}', 31 * 128 * 2, PH['conv'], PH['conv2'])
    for nm in ('mean', 'm2', 'rstd', 'yt'):
        A.add('cv_' + nm, 512 * 4, PH['conv2'], PH['conv2'])
    A.add('mixTc', 4 * TS * 2, PH['conv2'], PH['wout'])
    A.add('Z', 4 * TS * 4, PH['g2p'], PH['fin'])
    A.add('OT2', 4 * TS * 4, PH['g2p'], PH['fin'])
    A.add('OT1', 4 * TS * 4, PH['g1a'], PH['fin'])
    A.add('OT0', 4 * TS * 4, PH['g0a'], PH['fin'])
    A.add('qT2', 4 * TS * 2, PH['g2p'], PH['g2a'])
    A.add('kT2o', 4 * TS * 2, PH['g2p'], PH['g2a'])
    A.add('V2o', 9 * 512 * 2, PH['g2p'], PH['g2a'])
    A.add('qT1', 4 * TS * 2, PH['g1p'], PH['g1a'])
    A.add('kT1o', 4 * TS * 2, PH['g1p'], PH['g1a'])
    A.add('V1o', 9 * 512 * 2, PH['g1p'], PH['g1a'])
    A.add('kT1h', 4 * 512 * 2, PH['g1p'], PH['g1a'])
    A.add('V1h', 4 * 512 * 2, PH['g1p'], PH['g1a'])
    A.add('qT0', 4 * TS * 2, PH['g0p'], PH['g0a'])
    A.add('kT0o', 4 * TS * 2, PH['g0p'], PH['g0a'])
    A.add('V0o', 9 * 512 * 2, PH['g0p'], PH['g0a'])
    A.add('kT0h', 4 * 128 * 2, PH['g0p'], PH['g0a'])
    A.add('V0h', 1 * 512 * 2, PH['g0p'], PH['g0a'])
    for i in range(4):
        A.add(f'ET{i}', 384 * 2, PH['g2a'], PH['g0a'])
    for i in range(4):
        A.add(f'kvcb{i}', 1024 * 2, PH['g2a'], PH['g0a'])
        A.add(f'kTc{i}', 512 * 2, PH['g2a'], PH['g0a'])
    A.add('mixTa', 12 * TS * 2, PH['fin'], PH['wout'])
    for i in range(3):
        A.add(f'x1p{i}', D * 4, PH['wout'], PH['wout'])
    A.add('Wo2', 16384, PH['g0a'], PH['wout'])
    A.add('Wo3', 16384, PH['g0a'], PH['wout'])
    A.add('x1T', KT * TS * 2, PH['wout'], PH['ffn'])
    A.add('lng', D * 4, PH['wout'], PH['wout'])
    A.add('lnb', D * 4, PH['wout'], PH['wout'])
    for i in range(2):
        A.add(f'xp{i}', D * 4, PH['wout'], PH['wout'])
        A.add(f'x1b{i}', D * 2, PH['wout'], PH['wout'])
    A.add('stat', 18 * 8 * 4, PH['wout'], PH['ln2'])
    A.add('ls1', 9 * 8 * 4, PH['wdn'], PH['ln2'])
    A.add('ls2', 9 * 8 * 4, PH['wdn'], PH['ln2'])
    A.add('ljunk', 256 * 2, PH['wdn'], PH['wdn'])
    A.add('ljunk2', 256 * 2, PH['wdn'], PH['wdn'])
    A.add('fT', NFC * 1152 * 2, PH['ffn'], PH['wdn'])
    for i in range(2):
        A.add(f'sgf{i}', 512 * 4, PH['ffn'], PH['ffn'])
    for i in range(2):
        A.add(f'dxb{i}', 9 * 256 * 4, PH['wdn'], PH['wdn'])
        A.add(f'dyb{i}', 9 * 256 * 4, PH['wdn'], PH['wdn'])
    A.add('lng2', D * 4, PH['wdn'], PH['ln2'])
    A.add('lnb2', D * 4, PH['wdn'], PH['ln2'])
    for i in range(5):
        A.add(f'yt{i}', D * 4, PH['ln2'], PH['ln2'])
    A.add('lnjunk2', D * 4, PH['ln2'], PH['ln2'])
    asize = A.solve()
    assert asize <= 212800, asize

    st = ExitStack()
    with st:
        arena = st.enter_context(nc.sbuf_tensor("arena", [128, asize], U8))

        def buf(name, shape, dt, parts=128):
            nb = int(np.prod(shape)) * (4 if dt == F32 else 2)
            off = A.off[name]
            v = arena[0:parts, off:off + nb].bitcast(dt)
            if len(shape) == 2:
                v = v.rearrange("p (a b) -> p a b", a=shape[0])
            elif len(shape) == 3:
                v = v.rearrange("p (a b c) -> p a b c", a=shape[0], b=shape[1])
            return v

        accs = [st.enter_context(nc.psum_tensor(f"acc{i}", [128, 512], F32)) for i in range(4)]
        pTs = [st.enter_context(nc.psum_tensor(f"pT{i}", [128, 1024], BF16)) for i in range(2)]
        pSs = [st.enter_context(nc.psum_tensor(f"pS{i}", [128, 512], F32)) for i in range(2)]

        ident_f = buf('ident_f', [128], F32)
        ident_b = buf('ident_b', [128], BF16)
        ones_b = buf('ones_b', [128], BF16)
        ones_f = buf('ones_f', [128], F32)
        masks = buf('masks', [NMASK, 128], BF16)
        cw = buf('cw', [4, 31], F32)
        cv = buf('cvec', [12], F32)
        Wb = [buf('W0', [8192], BF16), buf('W1', [8192], BF16)]
        tabs = [buf(f'tab{i}', [256], F32) for i in range(2)]
        kfs = [buf(f'kf{i}', [512], F32) for i in range(2)]
        qbs = [buf(f'qb{i}', [512], BF16) for i in range(3)]
        tA = buf('tA', [512], F32)
        tB = buf('tB', [512], F32)

        S = Sched(nc)
        ctr = dict(acc=0, pT=0, tab=0, kf=0, qb=0, w=0, xb=0)

        def nxt(k, m):
            v = ctr[k] % m
            ctr[k] += 1
            return v

        S.dma('sp', 'c_id', lambda e: e.dma_start(out=ident_f, in_=ident_d), w=['ident_f'])
        S.dma('sp', 'c_cw', lambda e: e.dma_start(out=cw, in_=conv_wT.rearrange("(c p) j -> p c j", p=128)), w=['cw'])
        S.dma('sp', 'c_cv', lambda e: e.dma_start(out=cv, in_=cvec), w=['cvec'])
        S.dma('pool', 'c1', lambda e: e.dma_start(out=masks, in_=masks_d), w=['masks'])
        S.op('dve', lambda e: e.tensor_copy(ident_b, ident_f), r=['ident_f'], w=['ident_b'])
        S.op('dve', lambda e: e.memset(ones_b, 1.0), w=['ones_b'])
        S.op('dve', lambda e: e.memset(ones_f, 1.0), w=['ones_f'])

        def load_w(src3, shape3):
            slot = nxt('w', 2)
            a, b = shape3
            v = Wb[slot][:, 0:a * b].rearrange("p (a b) -> p a b", a=a)
            key = f'W{slot}'
            S.dma('pool', key, lambda e: e.dma_start(out=v, in_=src3), w=[key])
            return v, key

        def build_xt(src_rows, M, xbuf, xkey, dest_fn, dkey, stream):
            S.dma('pool', stream, lambda e: e.dma_start(out=xbuf[0:M, :], in_=src_rows), w=[xkey])
            for half in range(2):
                pi = nxt('pT', 2)
                pT = pTs[pi]
                for j in range(8):
                    kt = half * 8 + j
                    S.op('pe', lambda e, kt=kt, j=j, pT=pT: e.transpose(pT[:, j * 128:j * 128 + M], xbuf[0:M, kt * 128:(kt + 1) * 128], ident_b[0:M, 0:M]),
                         r=[xkey, 'ident_b'], w=[f'pT{pi}'])
                src = pT[:, :].rearrange("p (a b) -> p a b", a=8)[:, :, 0:M]
                dst = dest_fn(half * 8)
                if half == 0:
                    S.op('act', lambda e, src=src, dst=dst: e.copy(out=dst, in_=src), r=[f'pT{pi}'], w=[dkey])
                else:
                    S.op('dve', lambda e, src=src, dst=dst: e.tensor_copy(dst, src), r=[f'pT{pi}'], w=[dkey])

        def proj_tm(lhsT_fn, M, Wv, wkey, xkeys, ncols=512):
            ai = nxt('acc', 4)
            acc = accs[ai]
            for kt in range(KT):
                S.op('pe', lambda e, kt=kt, acc=acc: e.matmul(acc[0:M, 0:ncols], lhsT=lhsT_fn(kt), rhs=Wv[:, kt, 0:ncols], start=(kt == 0), stop=(kt == KT - 1)),
                     r=list(xkeys) + [wkey], w=[f'acc{ai}'])
            return acc, f'acc{ai}'

        def rope(acc, akey, M, tabidx, out_f32, out_bf, okeys):
            ti = nxt('tab', 2)
            tab = tabs[ti]
            S.dma('sp', f'tab{ti}', lambda e: e.dma_start(out=tab, in_=rope_d[tabidx]), w=[f'tab{ti}'])
            a3 = acc[0:M, :].rearrange("p (h d) -> p h d", h=4)
            tA3 = tA[0:M, :].rearrange("p (h d) -> p h d", h=4)
            tB3 = tB[0:M, :].rearrange("p (h d) -> p h d", h=4)
            cos2 = tab[0:M, 0:128].unsqueeze(1).to_broadcast([M, 4, 128])
            sa = tab[0:M, 128:192].unsqueeze(1).to_broadcast([M, 4, 64])
            sb = tab[0:M, 192:256].unsqueeze(1).to_broadcast([M, 4, 64])
            S.op('dve', lambda e: e.tensor_tensor(tA3, a3, cos2, op=ALU.mult), r=[akey, f'tab{ti}'], w=['tA'])
            S.op('dve', lambda e: e.tensor_tensor(tB3[:, :, 0:64], a3[:, :, 64:128], sa, op=ALU.mult), r=[akey, f'tab{ti}'], w=['tB'])
            S.op('dve', lambda e: e.tensor_tensor(tB3[:, :, 64:128], a3[:, :, 0:64], sb, op=ALU.mult), r=[akey, f'tab{ti}'], w=['tB'])
            if out_f32 is not None:
                S.op('pool', lambda e: e.tensor_tensor(out_f32, tA[0:M, :], tB[0:M, :], op=ALU.add), r=['tA', 'tB'], w=[okeys[0]])
                S.op('act', lambda e: e.copy(out=out_bf, in_=out_f32), r=[okeys[0]], w=[okeys[1]])
            else:
                S.op('pool', lambda e: e.tensor_tensor(out_bf, tA[0:M, :], tB[0:M, :], op=ALU.add), r=['tA', 'tB'], w=[okeys[1]])

        def tr4(src_bf, skey, M, dst, dkey):
            pi = nxt('pT', 2)
            pT = pTs[pi]
            for h in range(4):
                S.op('pe', lambda e, h=h: e.transpose(pT[:, h * 128:h * 128 + M], src_bf[0:M, h * 128:(h + 1) * 128], ident_b[0:M, 0:M]),
                     r=[skey, 'ident_b'], w=[f'pT{pi}'])
            src = pT[:, 0:512].rearrange("p (a b) -> p a b", a=4)[:, :, 0:M]
            S.op('dve', lambda e: e.tensor_copy(dst, src), r=[f'pT{pi}'], w=[dkey])

        deferred = []

        def flush_deferred(keep=0):
            while len(deferred) > keep:
                deferred.pop(0)()

        def q_tile(lhsT_fn, M, Wv, wkey, xkeys, tabidx, dst, dkey):
            acc, akey = proj_tm(lhsT_fn, M, Wv, wkey, xkeys)
            qi = nxt('qb', 3)
            rope(acc, akey, M, tabidx, None, qbs[qi][0:M, :], [None, f'qb{qi}'])
            flush_deferred(1)
            deferred.append(lambda: tr4(qbs[qi], f'qb{qi}', M, dst, dkey))

        def k_tile(lhsT_fn, M, Wv, wkey, xkeys, tabidx, dst, dkey, out_rows):
            acc, akey = proj_tm(lhsT_fn, M, Wv, wkey, xkeys)
            qi = nxt('qb', 3)
            if out_rows is None:
                rope(acc, akey, M, tabidx, None, qbs[qi][0:M, :], [None, f'qb{qi}'])
            else:
                ki = nxt('kf', 2)
                rope(acc, akey, M, tabidx, kfs[ki][0:M, :], qbs[qi][0:M, :], [f'kf{ki}', f'qb{qi}'])
                for (p0, np_, drows) in out_rows:
                    S.dma('pool', f'kf{ki}', lambda e, p0=p0, np_=np_, drows=drows: e.dma_start(out=drows, in_=kfs[ki][p0:p0 + np_, :]), r=[f'kf{ki}'])
            flush_deferred(1)
            deferred.append(lambda: tr4(qbs[qi], f'qb{qi}', M, dst, dkey))

        def v_tile(lhsT_fn, M, Wv, wkey, xkeys, dst_bf, dkey, out_rows):
            acc, akey = proj_tm(lhsT_fn, M, Wv, wkey, xkeys)
            flush_deferred(0)
            if out_rows is None:
                S.op('act', lambda e: e.copy(out=dst_bf, in_=acc[0:M, :]), r=[akey], w=[dkey])
            else:
                ki = nxt('kf', 2)
                S.op('act', lambda e: e.copy(out=kfs[ki][0:M, :], in_=acc[0:M, :]), r=[akey], w=[f'kf{ki}'])
                S.op('dve', lambda e: e.tensor_copy(dst_bf, kfs[ki][0:M, :]), r=[f'kf{ki}'], w=[dkey])
                for (p0, np_, drows) in out_rows:
                    S.dma('pool', f'kf{ki}', lambda e, p0=p0, np_=np_, drows=drows: e.dma_start(out=drows, in_=kfs[ki][p0:p0 + np_, :]), r=[f'kf{ki}'])

        def wblk(col0):
            return w_in[:, col0:col0 + 512].rearrange("(kt p) c -> p kt c", p=128)

        XTH = buf('XTH', [KT, HALO], BF16)
        kT2h = buf('kT2h', [4, 2048], BF16)
        V2h = buf('V2h', [16, 512], BF16)
        xbb = [buf(f'xb{i}', [D], BF16) for i in range(2)]
        for ht in range(16):
            xi = nxt('xb', 2)
            build_xt(x_halo[ht * 128:(ht + 1) * 128, :], 128, xbb[xi], f'xb{xi}',
                     lambda kt0, ht=ht: XTH[:, kt0:kt0 + 8, ht * 128:(ht + 1) * 128], 'XTH', f'xb{xi}')
        Wk, wk_key = load_w(wblk(1536 + 1024), (KT, 512))
        Wv_, wv_key = load_w(wblk(3072 + 1024), (KT, 512))
        for r in range(16):
            k_tile(lambda kt, r=r: XTH[:, kt, r:HALO:16], 128, Wk, wk_key, ['XTH'], TAB_G[2] + 8 + r,
                   kT2h[:, :, r * 128:(r + 1) * 128], 'kT2h', None)
        for r in range(16):
            v_tile(lambda kt, r=r: XTH[:, kt, r:HALO:16], 128, Wv_, wv_key, ['XTH'], V2h[:, r, :], 'V2h', None)
        flush_deferred(0)
        pre = {}
        S.barrier()
        if stop == 'halo':
            S.emit(st)
            return nc

        XTO = buf('XTO', [KT, TOK], BF16)
        XTS = buf('XTS', [KT, 4], BF16)
        XTH1 = buf('XTH1', [KT, 512], BF16)
        XTHc = buf('XTHc', [KT, 128], BF16)
        xbs = buf('xbs', [D], BF16)
        for t in range(NTILE):
            xi = nxt('xb', 2)
            build_xt(x_own[t * 128:(t + 1) * 128, :], 128, xbb[xi], f'xb{xi}',
                     lambda kt0, t=t: XTO[:, kt0:kt0 + 8, t * 128:(t + 1) * 128], 'XTO', f'xb{xi}')
        xi = nxt('xb', 2)
        build_xt(x_halo[1920:2048, :], 128, xbb[xi], f'xb{xi}', lambda kt0: XTHc[:, kt0:kt0 + 8, :], 'XTHc', f'xb{xi}')
        build_xt(x_s, 4, xbs, 'xbs', lambda kt0: XTS[:, kt0:kt0 + 8, :], 'XTS', 'xbs')
        XK = ['XTO', 'XTS', 'XTH1', 'XTHc']
        pre['a'] = load_w(wblk(4608), (KT, 512))
        pre['g'] = load_w(wblk(5120), (KT, 512))
        S.barrier()

        uT = buf('uT', [4, 1054], F32)
        uTs = buf('uTs', [4, 34], F32)
        uTb = buf('uTb', [4, 1054], BF16)
        uTsb = buf('uTsb', [4, 34], BF16)
        dgs = [buf(f'dg{i}', [31, 128], BF16) for i in range(4)]
        cT = buf('cT', [4, TS], F32)
        sq = buf('sq', [4, 512], F32)
        scv = buf('scv', [512], F32)
        cvo = buf('cvo', [512], F32)
        cvo2 = buf('cvo2', [512], F32)
        sgs = [buf(f'sg{i}', [512], F32) for i in range(2)]
        cmean = buf('cv_mean', [512], F32)
        cm2 = buf('cv_m2', [512], F32)
        crstd = buf('cv_rstd', [512], F32)
        cyt = buf('cv_yt', [512], F32)
        mixTc = buf('mixTc', [4, TS], BF16)
        nbs = [('o0', lambda kt: XTO[:, kt, 0:512], 512, lambda c: uT[:, c, 30:542], lambda c: uTb[:, c, 30:542]),
               ('o1', lambda kt: XTO[:, kt, 512:1024], 512, lambda c: uT[:, c, 542:1054], lambda c: uTb[:, c, 542:1054]),
               ('h', lambda kt: XTHc[:, kt, 98:128], 30, lambda c: uT[:, c, 0:30], lambda c: uTb[:, c, 0:30]),
               ('s', lambda kt: XTS[:, kt, 0:4], 4, lambda c: uTs[:, c, 30:34], lambda c: uTsb[:, c, 30:34])]
        Wa, wa_key = pre.pop('a')
        Wg, wg_key = pre.pop('g')
        sgc = [0]
        for which, (Wx, wx_key) in enumerate([(Wa, wa_key), (Wg, wg_key)]):
            for c in range(4):
                for (nm, rhs_fn, N, dst_fn, dstb_fn) in nbs:
                    ai = nxt('acc', 4)
                    acc = accs[ai]
                    for kt in range(KT):
                        S.op('pe', lambda e, kt=kt, acc=acc, rhs_fn=rhs_fn, N=N, Wx=Wx, c=c: e.matmul(acc[:, 0:N], lhsT=Wx[:, kt, c * 128:(c + 1) * 128], rhs=rhs_fn(kt), start=(kt == 0), stop=(kt == KT - 1)),
                             r=XK + [wx_key], w=[f'acc{ai}'])
                    dst = dst_fn(c)
                    ukey = f'u{c}{nm}'
                    if which == 0:
                        S.op('act', lambda e, dst=dst, acc=acc, N=N: e.copy(out=dst, in_=acc[:, 0:N]), r=[f'acc{ai}'], w=[ukey])
                    else:
                        si = sgc[0] % 2
                        sgc[0] += 1
                        sg = sgs[si]
                        S.op('act', lambda e, sg=sg, acc=acc, N=N: e.activation(out=sg[:, 0:N], in_=acc[:, 0:N], func=AF.Sigmoid), r=[f'acc{ai}'], w=[f'sg{si}'])
                        S.op('dve', lambda e, dst=dst, sg=sg, N=N: e.tensor_tensor(dst, dst, sg[:, 0:N], op=ALU.mult), r=[f'sg{si}', ukey], w=[ukey])
                        dstb = dstb_fn(c)
                        S.op('pool', lambda e, dst=dst, dstb=dstb: e.tensor_copy(dstb, dst), r=[ukey], w=[ukey + 'b'])
        for c in range(4):
            S.op('dve', lambda e, c=c: e.tensor_tensor(dgs[c], ident_f.unsqueeze(1).to_broadcast([128, 31, 128]), cw[:, c, :].unsqueeze(2).to_broadcast([128, 31, 128]), op=ALU.mult),
                 r=['ident_f', 'cw'], w=[f'dg{c}'])
        pre[('q', 2)] = load_w(wblk(512 * 2), (KT, 512))
        pre[('k', 2)] = load_w(wblk(1536 + 512 * 2), (KT, 512))
        S.dma('sp', 'scv', lambda e: e.dma_start(out=scv[0:30, :], in_=sconv), w=['scv'])
        S.dma('sp', 'd2d', lambda e: e.dma_start(out=convs_out[0:26, :], in_=sconv[4:30, :]))
        for c in range(4):
            S.op('pe', lambda e, c=c: e.transpose(pSs[0][:, c * 32:c * 32 + 30], scv[0:30, c * 128:(c + 1) * 128], ident_f[0:30, 0:30]), r=['scv', 'ident_f'], w=['pS0'])
        S.op('dve', lambda e: e.tensor_copy(uTs[:, :, 0:30], pSs[0][:, 0:128].rearrange("p (c j) -> p c j", c=4)[:, :, 0:30]), r=['pS0'], w=['uTs_h'])
        S.op('pool', lambda e: e.tensor_copy(uTsb[:, :, 0:30], uTs[:, :, 0:30]), r=['uTs_h'], w=['uTs_hb'])
        ukeys_all = [f'u{c}{nm}' for c in range(4) for nm in ('o0', 'o1', 'h', 's')] + ['uTs_h']
        for c in range(4):
            S.op('pe', lambda e, c=c: e.transpose(pSs[1][0:30, c * 128:(c + 1) * 128], uT[:, c, 1024:1054], ident_f[:, :]), r=ukeys_all + ['ident_f'], w=['pS1'])
        S.op('dve', lambda e: e.tensor_copy(cvo[0:30, :], pSs[1][0:30, :]), r=['pS1'], w=['cvo'])
        S.dma('sp', 'cvo', lambda e: e.dma_start(out=conv_out, in_=cvo[0:30, :]), r=['cvo'])
        for c in range(4):
            S.op('pe', lambda e, c=c: e.transpose(pSs[0][0:4, c * 128:(c + 1) * 128], uTs[:, c, 30:34], ident_f[:, :]), r=ukeys_all + ['ident_f'], w=['pS0'])
        S.op('dve', lambda e: e.tensor_copy(cvo2[0:4, :], pSs[0][0:4, :]), r=['pS0'], w=['cvo2'])
        S.dma('sp', 'cvo', lambda e: e.dma_start(out=convs_out[26:30, :], in_=cvo2[0:4, :]), r=['cvo2'])
        S.barrier()
        ubkeys = [f'u{c}{nm}b' for c in range(4) for nm in ('o0', 'o1', 'h', 's')] + ['uTs_hb']
        dgc = [0]
        for c in range(4):
            b0, b1, bs_ = (accs[0], accs[1], pSs[0]) if c % 2 == 0 else (accs[2], accs[3], pSs[1])
            k0, k1, ks_ = ('acc0', 'acc1', 'pS0') if c % 2 == 0 else ('acc2', 'acc3', 'pS1')
            di = c
            dgA = dgs[di]
            for j in range(31):
                dg = dgA[:, j, :]
                S.op('pe', lambda e, dg=dg, c=c, j=j, b0=b0: e.matmul(b0[:, 0:512], lhsT=dg, rhs=uTb[:, c, j:j + 512], start=(j == 0), stop=(j == 30)), r=ubkeys + [f'dg{di}'], w=[k0])
                S.op('pe', lambda e, dg=dg, c=c, j=j, b1=b1: e.matmul(b1[:, 0:512], lhsT=dg, rhs=uTb[:, c, 512 + j:1024 + j], start=(j == 0), stop=(j == 30)), r=ubkeys + [f'dg{di}'], w=[k1])
                S.op('pe', lambda e, dg=dg, c=c, j=j, bs_=bs_: e.matmul(bs_[:, 0:4], lhsT=dg, rhs=uTsb[:, c, j:j + 4], start=(j == 0), stop=(j == 30)), r=ubkeys + [f'dg{di}'], w=[ks_])
            S.op('act', lambda e, c=c, b0=b0: e.activation(out=cT[:, c, 0:512], in_=b0[:, 0:512], func=AF.Identity, bias=cv[:, c:c + 1]), r=[k0, 'cvec'], w=[f'cT{c}'])
            S.op('act', lambda e, c=c, b1=b1: e.activation(out=cT[:, c, 512:1024], in_=b1[:, 0:512], func=AF.Identity, bias=cv[:, c:c + 1]), r=[k1, 'cvec'], w=[f'cT{c}'])
            S.op('act', lambda e, c=c, bs_=bs_: e.activation(out=cT[:, c, TOK:TS], in_=bs_[:, 0:4], func=AF.Identity, bias=cv[:, c:c + 1]), r=[ks_, 'cvec'], w=[f'cTs{c}'])
        ckeys = [f'cT{c}' for c in range(4)] + [f'cTs{c}' for c in range(4)]
        sqkeys = [f'sq{c}' for c in range(4)]
        def f32view(name, n):
            off = A.off[name]
            return arena[:, off:off + n * 4].bitcast(F32)
        cmean_all = f32view('dg0', TS)
        crstd_all = f32view('dg1', TS)
        cyt2 = f32view('dg2', 512)
        blocks = ((0, 512), (512, 512), (1024, 4))
        for (c0, N) in blocks:
            for c in range(4):
                S.op('act', lambda e, c=c, c0=c0, N=N: e.activation(out=sq[:, c, 0:N], in_=cT[:, c, c0:c0 + N], func=AF.Square), r=ckeys, w=[f'sq{c}'])
            for (srcT, bank, kk, o0) in ((cT, 0, ckeys, c0), (sq, 1, sqkeys, 0)):
                for c in range(4):
                    S.op('pe', lambda e, srcT=srcT, bank=bank, c=c, o0=o0, N=N: e.matmul(pSs[bank][:, 0:N], lhsT=ones_f[:, :], rhs=srcT[:, c, o0:o0 + N], start=(c == 0), stop=(c == 3)),
                         r=kk + ['ones_f'], w=[f'pS{bank}'])
            cm_ = cmean_all[:, c0:c0 + N]
            cr_ = crstd_all[:, c0:c0 + N]
            S.op('dve', lambda e, N=N, cm_=cm_: e.tensor_scalar(cm_, pSs[0][:, 0:N], 1.0 / 512, None, op0=ALU.mult), r=['pS0', 'dg0'], w=['dg0'])
            S.op('dve', lambda e, N=N, cm_=cm_: e.tensor_tensor(cm2[:, 0:N], cm_, cm_, op=ALU.mult), r=['dg0'], w=['cm2'])
            S.op('dve', lambda e, N=N: e.scalar_tensor_tensor(cm2[:, 0:N], pSs[1][:, 0:N], 1.0 / 512, cm2[:, 0:N], op0=ALU.mult, op1=ALU.subtract), r=['pS1', 'cm2'], w=['cm2'])
            S.op('dve', lambda e, N=N: e.tensor_scalar(cm2[:, 0:N], cm2[:, 0:N], EPS, None, op0=ALU.add), r=['cm2'], w=['cm2'])
            S.op('act', lambda e, N=N: e.activation(out=cm2[:, 0:N], in_=cm2[:, 0:N], func=AF.Ln), r=['cm2'], w=['cm2'])
            S.op('act', lambda e, N=N, cr_=cr_: e.activation(out=cr_, in_=cm2[:, 0:N], func=AF.Exp, scale=-0.5), r=['cm2', 'dg1'], w=['dg1'])
        cnt_ = [0]
        for (c0, N) in blocks:
            cm_ = cmean_all[:, c0:c0 + N]
            cr_ = crstd_all[:, c0:c0 + N]
            for c in range(4):
                yb, yk = ((cyt, 'cyt'), (cyt2, 'dg2'))[cnt_[0] % 2]
                cnt_[0] += 1
                S.op('dve', lambda e, c=c, c0=c0, N=N, yb=yb, cm_=cm_: e.tensor_tensor(yb[:, 0:N], cT[:, c, c0:c0 + N], cm_, op=ALU.subtract), r=ckeys + ['dg0', yk], w=[yk])
                S.op('dve', lambda e, N=N, yb=yb, cr_=cr_: e.tensor_tensor(yb[:, 0:N], yb[:, 0:N], cr_, op=ALU.mult), r=[yk, 'dg1'], w=[yk])
                S.op('dve', lambda e, c=c, N=N, yb=yb: e.tensor_scalar(yb[:, 0:N], yb[:, 0:N], cv[:, 4 + c:5 + c], cv[:, 8 + c:9 + c], op0=ALU.mult, op1=ALU.add), r=[yk, 'cvec'], w=[yk])
                S.op('act', lambda e, c=c, c0=c0, N=N, yb=yb: e.activation(out=mixTc[:, c, c0:c0 + N], in_=yb[:, 0:N], func=AF.Silu), r=[yk], w=['mixTc'])
        S.barrier()
        if stop == 'conv':
            S.emit(st)
            return nc

        Zt = buf('Z', [4, TS], F32)
        OTs = {g: buf(f'OT{g}', [4, TS], F32) for g in range(3)}
        ETs = [buf(f'ET{i}', [384], BF16) for i in range(4)]
        kvcb = [buf(f'kvcb{i}', [2, 512], BF16) for i in range(4)]
        kTc = [buf(f'kTc{i}', [4, 128], BF16) for i in range(4)]
        actr = dict(it=0, et=0, kvc=0, slot=0)

        def attn_A(it):
            g, nq, qT_ap, chunks = it['g'], it['nq'], it['qT'], it['chunks']
            slot = actr['slot'] % 4
            actr['slot'] += 1
            it['slot'] = slot
            sbank = accs[slot]
            skey = f'acc{slot}'
            ET = ETs[slot]
            for ci, ch in enumerate(chunks):
                nk = ch['nk']
                col = ci * (128 if nq == 128 else 8)
                S.op('pe', lambda e, ch=ch, nk=nk, col=col: e.matmul(sbank[0:nk, col:col + nq], lhsT=ch['kT'], rhs=qT_ap, start=True, stop=False, skip_group_check=True),
                     r=ch['keys'] + [f'qT{g}'], w=[skey])
                S.op('pe', lambda e, ch=ch, nk=nk, col=col: e.matmul(sbank[0:nk, col:col + nq], lhsT=ident_b[0:nk, 0:nk], rhs=ch['mask'], start=False, stop=True, skip_group_check=True),
                     r=['masks', 'ident_b'], w=[skey])
            if nq == 128 and all(ch['nk'] == 128 for ch in chunks):
                ncol = len(chunks) * 128
                S.op('act', lambda e: e.activation(out=ET[:, 0:ncol], in_=sbank[:, 0:ncol], func=AF.Exp, scale=SCALE), r=[skey], w=[f'ET{slot}'])
            else:
                for ci, ch in enumerate(chunks):
                    nk = ch['nk']
                    col = ci * (128 if nq == 128 else 8)
                    S.op('act', lambda e, nk=nk, col=col: e.activation(out=ET[0:nk, col:col + nq], in_=sbank[0:nk, col:col + nq], func=AF.Exp, scale=SCALE), r=[skey], w=[f'ET{slot}'])

        def attn_B(it):
            g, nq, chunks, ot_dst, z_dst, first_group = it['g'], it['nq'], it['chunks'], it['ot'], it['z'], it['first']
            slot = it['slot']
            ET = ETs[slot]
            par = actr['it'] % 2
            actr['it'] += 1
            obank = pSs[par]
            okey = f'pS{par}'
            n = len(chunks)
            for ci, ch in enumerate(chunks):
                nk = ch['nk']
                col = ci * (128 if nq == 128 else 8)
                S.op('pe', lambda e, ch=ch, nk=nk, col=col, fo=(ci == 0): e.matmul(obank[:, 0:nq], lhsT=ch['V'], rhs=ET[0:nk, col:col + nq], start=fo, stop=False, skip_group_check=True),
                     r=ch['keys'] + [f'ET{slot}'], w=[okey])
                S.op('pe', lambda e, nk=nk, col=col, last=(ci == n - 1): e.matmul(obank[:, 128:128 + nq], lhsT=ones_b[0:nk, :], rhs=ET[0:nk, col:col + nq], start=False, stop=last, skip_group_check=True),
                     r=[f'ET{slot}', 'ones_b'], w=[okey])
            S.op('act', lambda e: e.copy(out=ot_dst, in_=obank[:, 0:nq]), r=[okey], w=[f'OT{g}', f'ord{par}'])
            if first_group:
                S.op('dve', lambda e: e.tensor_copy(z_dst, obank[:, 128:128 + nq]), r=[okey, f'ord{par}'], w=['Z'])
            else:
                S.op('dve', lambda e: e.tensor_tensor(z_dst, z_dst, obank[:, 128:128 + nq], op=ALU.add), r=[okey, 'Z', f'ord{par}'], w=['Z'])

        def run_attention(its, hook_at=None, hook=None):
            n = len(its)
            for i in range(min(2, n)):
                attn_A(its[i])
            for i in range(n):
                attn_B(its[i])
                if hook is not None and i == hook_at:
                    hook()
                if i + 2 < n:
                    attn_A(its[i + 2])

        def _v(ps_ap, like):
            if len(like.shape) == 3:
                return ps_ap.rearrange("p (a b) -> p a b", a=like.shape[1])
            return ps_ap

        def own_lhsT(g, ti):
            if g == 0:
                return lambda kt: XTO[:, kt, ti * 128:(ti + 1) * 128]
            if g == 1:
                r4, n = ti // 2, ti % 2
                return lambda kt: XTO[:, kt, 512 * n + r4:512 * n + r4 + 509:4]
            return lambda kt: XTO[:, kt, ti:ti + 1017:8]

        def own_rows(g, ti, dram, kvsel):
            if g == 0:
                return [(0, 128, dram[ti * 128:(ti + 1) * 128, kvsel, :])]
            if g == 1:
                r4, n = ti // 2, ti % 2
                return [(0, 128, dram[512 * n + r4:512 * n + r4 + 509:4, kvsel, :])]
            return [(0, 128, dram[ti:ti + 1017:8, kvsel, :])]

        def own_cols(g, ti, T3, h):
            if g == 0:
                return T3[:, h, ti * 128:(ti + 1) * 128]
            if g == 1:
                r4, n = ti // 2, ti % 2
                return T3[:, h, 512 * n + r4:512 * n + r4 + 509:4]
            return T3[:, h, ti:ti + 1017:8]

        dil = {0: 1, 1: 4, 2: 16}
        nhalo = {0: 1, 1: 4, 2: 16}
        for g in (2, 1, 0):
            qT = buf(f'qT{g}', [4, TS], BF16)
            kTo = buf(f'kT{g}o', [4, TS], BF16)
            Vo = buf(f'V{g}o', [9, 512], BF16)
            if g == 2:
                kTh, Vh = kT2h, V2h
            else:
                kTh = buf(f'kT{g}h', [4, nhalo[g] * 128], BF16)
                Vh = buf(f'V{g}h', [nhalo[g], 512], BF16)
            if g == 1:
                for t in range(4):
                    build_xt(x_halo[1536 + t * 128:1536 + (t + 1) * 128, :], 128, xbb[0], 'xb0',
                             lambda kt0, t=t: XTH1[:, kt0:kt0 + 8, t * 128:(t + 1) * 128], 'XTH1', 'xb0')
            Wq, wq_key = pre.pop(('q', g)) if ('q', g) in pre else load_w(wblk(512 * g), (KT, 512))
            Wk, wk_key = pre.pop(('k', g)) if ('k', g) in pre else load_w(wblk(1536 + 512 * g), (KT, 512))
            for ti in range(NTILE):
                q_tile(own_lhsT(g, ti), 128, Wq, wq_key, XK, TAB_G[g] + ti, qT[:, :, ti * 128:(ti + 1) * 128], f'qT{g}')
            if 'A' not in SKIP:
                q_tile(lambda kt: XTS[:, kt, 0:4], 4, Wq, wq_key, XK, TAB_S, qT[:, :, TOK:TS], f'qT{g}')
            Wv_, wv_key = load_w(wblk(3072 + 512 * g), (KT, 512))
            for ti in range(NTILE):
                k_tile(own_lhsT(g, ti), 128, Wk, wk_key, XK, TAB_G[g] + ti, kTo[:, :, ti * 128:(ti + 1) * 128], f'kT{g}o', None if ('B' in SKIP or 'K' in SKIP) else own_rows(g, ti, kvout[g], 0))
            nb_ = kvc_in[g].shape[0]
            if 'A' not in SKIP:
                k_tile(lambda kt: XTS[:, kt, 0:4], 4, Wk, wk_key, XK, TAB_S, kTo[:, :, TOK:TS], f'kT{g}o', [(0, 4, kvs_out[g][nb_ - 4:nb_, 0, :])])
            if g == 1:
                for r in range(4):
                    k_tile(lambda kt, r=r: XTH1[:, kt, r:512:4], 128, Wk, wk_key, XK, TAB_G[1] + 8 + r, kTh[:, :, r * 128:(r + 1) * 128], 'kT1h', None)
            if g == 0:
                k_tile(lambda kt: XTH1[:, kt, 384:512], 128, Wk, wk_key, XK, TAB_G[0] + 8, kTh[:, :, 0:128], 'kT0h', None)
            for ti in range(NTILE):
                v_tile(own_lhsT(g, ti), 128, Wv_, wv_key, XK, Vo[:, ti, :], f'V{g}o', None if ('B' in SKIP or 'V' in SKIP) else own_rows(g, ti, kvout[g], 1))
            if 'A' not in SKIP:
                v_tile(lambda kt: XTS[:, kt, 0:4], 4, Wv_, wv_key, XK, Vo[0:4, 8, :], f'V{g}o', [(0, 4, kvs_out[g][nb_ - 4:nb_, 1, :])])
            if g == 1:
                for r in range(4):
                    v_tile(lambda kt, r=r: XTH1[:, kt, r:512:4], 128, Wv_, wv_key, XK, Vh[:, r, :], 'V1h', None)
            if g == 0:
                v_tile(lambda kt: XTH1[:, kt, 384:512], 128, Wv_, wv_key, XK, Vh[:, 0, :], 'V0h', None)
            flush_deferred(0)
            for r0_ in ([] if 'C' in SKIP else range(0, nb_ - 4, 128)):
                r1_ = min(r0_ + 128, nb_ - 4)
                S.dma('sp', 'd2d', lambda e, g=g, r0_=r0_, r1_=r1_: e.dma_start(out=kvs_out[g][r0_:r1_], in_=kvc_in[g][r0_ + 4:r1_ + 4]))
            S.barrier()
            d = dil[g]
            ntile_c = 1 if g == 0 else 4
            for tc_ in range(ntile_c):
                src = kvc_in[g][:, :, :] if g == 0 else kvc_in[g][tc_:nb_:d, :, :]
                S.dma('pool', f'kvcb{tc_}', lambda e, tc_=tc_, src=src: e.dma_start(out=kvcb[tc_], in_=src), w=[f'kvcb{tc_}'])
            if g > 0:
                pre[('q', g - 1)] = load_w(wblk(512 * (g - 1)), (KT, 512))
                pre[('k', g - 1)] = load_w(wblk(1536 + 512 * (g - 1)), (KT, 512))
            if g == 0:
                Wo = []
                for cb in range(4):
                    src3 = w_out[:, cb * 512:(cb + 1) * 512].rearrange("(kc p) c -> p kc c", p=128)
                    if cb < 2:
                        Wo.append(load_w(src3, (KT, 512)))
                    else:
                        wv_ = buf(f'Wo{cb}', [KT, 512], BF16)
                        S.dma('pool', f'Wo{cb}', lambda e, wv_=wv_, src3=src3: e.dma_start(out=wv_, in_=src3), w=[f'Wo{cb}'])
                        Wo.append((wv_, f'Wo{cb}'))

            if stop == f'g{g}p':
                S.emit(st)
                return nc
            OT = OTs[g]
            KO = [f'kT{g}o', f'V{g}o']
            KH = [f'kT{g}h', f'V{g}h']
            its = []
            for ti in range(NTILE):
                for h in range(4):
                    qT_ap = qT[:, h, ti * 128:(ti + 1) * 128]
                    hs = slice(h * 128, (h + 1) * 128)
                    same = dict(nk=128, kT=kTo[:, h, ti * 128:(ti + 1) * 128], V=Vo[:, ti, hs], mask=masks[:, 3 if g == 2 else 1, :], keys=KO)
                    if g == 2:
                        ha = dict(nk=128, kT=kTh[:, h, ti * 128:(ti + 1) * 128], V=Vh[:, ti, hs], mask=masks[:, 4, :], keys=KH)
                        hb = dict(nk=128, kT=kTh[:, h, (ti + 8) * 128:(ti + 9) * 128], V=Vh[:, ti + 8, hs], mask=masks[:, 12, :], keys=KH)
                        chunks = [same, ha, hb]
                    else:
                        if g == 0:
                            first = (ti == 0)
                            hidx = 0
                        else:
                            first = (ti % 2 == 0)
                            hidx = ti // 2
                        pt = ti - 1
                        if first:
                            prev = dict(nk=128, kT=kTh[:, h, hidx * 128:(hidx + 1) * 128], V=Vh[:, hidx, hs], mask=masks[:, 2, :], keys=KH)
                        else:
                            prev = dict(nk=128, kT=kTo[:, h, pt * 128:(pt + 1) * 128], V=Vo[:, pt, hs], mask=masks[:, 0, :], keys=KO)
                        chunks = [same, prev]
                    its.append(dict(g=g, nq=128, qT=qT_ap, chunks=chunks, ot=own_cols(g, ti, OT, h), z=own_cols(g, ti, Zt, h), first=(g == 2)))
            its_prompt = its
            if stop == f'g{g}ap':
                S.emit(st)
                return nc
            def cache_transposes(g=g, ntile_c=ntile_c):
                for tc_ in range(ntile_c):
                    ci = tc_
                    pi = nxt('pT', 2)
                    for h in range(4):
                        S.op('pe', lambda e, h=h, ci=ci, pi=pi: e.transpose(pTs[pi][:, h * 128:(h + 1) * 128], kvcb[ci][:, 0, h * 128:(h + 1) * 128], ident_b[:, :]), r=[f'kvcb{ci}', 'ident_b'], w=[f'pT{pi}'])
                    S.op('dve', lambda e, ci=ci, pi=pi: e.tensor_copy(kTc[ci], pTs[pi][:, 0:512].rearrange("p (a b) -> p a b", a=4)), r=[f'pT{pi}'], w=[f'kTc{ci}'])
            its = list(its_prompt)
            for h in range(4):
                hs = slice(h * 128, (h + 1) * 128)
                chunks = [dict(nk=4, kT=kTo[:, h, TOK:TS], V=Vo[0:4, 8, hs], mask=masks[0:4, 6 if g == 0 else 11, 0:4], keys=KO)]
                for tc_ in range(ntile_c):
                    chunks.append(dict(nk=128, kT=kTc[tc_][:, h, :], V=kvcb[tc_][:, 1, hs], mask=masks[:, 5 if g == 0 else 7 + tc_, 0:4], keys=[f'kTc{tc_}', f'kvcb{tc_}']))
                its.append(dict(g=g, nq=4, qT=qT[:, h, TOK:TS], chunks=chunks, ot=OT[:, h, TOK:TS], z=Zt[:, h, TOK:TS], first=(g == 2)))
            run_attention(its, hook_at=20, hook=cache_transposes)
            S.barrier()
            if stop == f'g{g}':
                S.emit(st)
                return nc

        mixTa = buf('mixTa', [12, TS], BF16)
        for h in range(4):
            S.op('act', lambda e, h=h: e.activation(out=Zt[:, h, :], in_=Zt[:, h, :], func=AF.Ln), r=['Z'], w=[f'Zr{h}'])
            S.op('act', lambda e, h=h: e.activation(out=Zt[:, h, :], in_=Zt[:, h, :], func=AF.Exp, scale=-1.0), r=[f'Zr{h}'], w=[f'Zr{h}'])
            for g in range(3):
                eng = 'dve'
                S.op(eng, lambda e, g=g, h=h: e.tensor_tensor(mixTa[:, 4 * g + h, :], OTs[g][:, h, :], Zt[:, h, :], op=ALU.mult), r=[f'OT{g}', f'Zr{h}'], w=[f'mixTa{g}{h}'])
        S.barrier()
        if stop == 'fin':
            S.emit(st)
            return nc

        stat = buf('stat', [18, 8], F32)
        lnc = [0]

        def layernorm(xap, M, gt, bt, junk, xkey, jkey='junk', pre_sums=None, act_norm=False, beta_eng='pool', defer_apply=False):
            si = lnc[0]
            lnc[0] += 1
            st_ = stat[0:M, si, :]
            sk = f'stat{si}'
            if pre_sums is None:
                S.op('dve', lambda e: e.memset(st_[:, 0:2], 0.0), w=[sk])
                S.op('act', lambda e: e.activation(out=junk[0:M, :], in_=xap, func=AF.Identity, accum_out=st_[:, 0:1]), r=[xkey, sk], w=[sk, jkey])
                S.op('act', lambda e: e.activation(out=junk[0:M, :], in_=xap, func=AF.Square, accum_out=st_[:, 1:2]), r=[xkey, sk], w=[sk, jkey])
            else:
                a1, a2, pk = pre_sums
                S.op('dve', lambda e: e.reduce_sum(out=st_[:, 0:1], in_=a1, axis=mybir.AxisListType.X), r=[pk], w=[sk])
                S.op('dve', lambda e: e.reduce_sum(out=st_[:, 1:2], in_=a2, axis=mybir.AxisListType.X), r=[pk, sk], w=[sk])
            S.op('dve', lambda e: e.tensor_scalar(st_[:, 2:3], st_[:, 0:1], 1.0 / D, None, op0=ALU.mult), r=[sk], w=[sk])
            S.op('dve', lambda e: e.tensor_tensor(st_[:, 3:4], st_[:, 2:3], st_[:, 2:3], op=ALU.mult), r=[sk], w=[sk])
            S.op('dve', lambda e: e.scalar_tensor_tensor(st_[:, 3:4], st_[:, 1:2], 1.0 / D, st_[:, 3:4], op0=ALU.mult, op1=ALU.subtract), r=[sk], w=[sk])
            S.op('dve', lambda e: e.tensor_scalar(st_[:, 3:4], st_[:, 3:4], EPS, None, op0=ALU.add), r=[sk], w=[sk])
            S.op('act', lambda e: e.activation(out=st_[:, 3:4], in_=st_[:, 3:4], func=AF.Sqrt), r=[sk], w=[sk])
            S.op('dve', lambda e: e.reciprocal(st_[:, 3:4], st_[:, 3:4]), r=[sk], w=[sk])
            S.op('dve', lambda e: e.scalar_tensor_tensor(st_[:, 4:5], st_[:, 2:3], -1.0, st_[:, 3:4], op0=ALU.mult, op1=ALU.mult), r=[sk], w=[sk])

            def apply():
                if act_norm:
                    S.op('act', lambda e: e.activation(out=xap, in_=xap, func=AF.Identity, scale=st_[:, 3:4], bias=st_[:, 4:5]), r=[xkey, sk], w=[xkey])
                else:
                    S.op('dve', lambda e: e.tensor_scalar(xap, xap, st_[:, 3:4], st_[:, 4:5], op0=ALU.mult, op1=ALU.add), r=[xkey, sk], w=[xkey])
                S.op('dve', lambda e: e.tensor_tensor(xap, xap, gt[0:M, :], op=ALU.mult), r=[xkey, 'lnp'], w=[xkey])
                S.op(beta_eng, lambda e: e.tensor_tensor(xap, xap, bt[0:M, :], op=ALU.add), r=[xkey, 'lnp'], w=[xkey])
            if defer_apply:
                return apply
            apply()

        def tile_rows(ti):
            return (ti * 128, 128) if ti < 8 else (TOK, 4)

        x1ps = [buf(f'x1p{i}', [D], F32) for i in range(3)]
        x1T = buf('x1T', [KT, TS], BF16)
        lng = buf('lng', [D], F32)
        lnb = buf('lnb', [D], F32)
        xps = [buf(f'xp{i}', [D], F32) for i in range(2)]
        x1bs = [buf(f'x1b{i}', [D], BF16) for i in range(2)]
        S.dma('sp', 'lnp', lambda e: e.dma_start(out=lng, in_=lnp[0].partition_broadcast(128)), w=['lnp'])
        S.dma('sp', 'lnp', lambda e: e.dma_start(out=lnb, in_=lnp[1].partition_broadcast(128)), w=['lnp'])
        ctr['xp'] = 0
        ctr['x1b'] = 0
        ctr['x1p'] = 0

        def x1_transposes(bi, r0, M):
            for half in range(2):
                pi = nxt('pT', 2)
                for j in range(8):
                    kt = half * 8 + j
                    S.op('pe', lambda e, kt=kt, j=j, pi=pi: e.transpose(pTs[pi][:, j * 128:j * 128 + M], x1bs[bi][0:M, kt * 128:(kt + 1) * 128], ident_b[0:M, 0:M]),
                         r=[f'x1b{bi}', 'ident_b'], w=[f'pT{pi}'])
                dst = x1T[:, half * 8:half * 8 + 8, r0:r0 + M]
                src = pTs[pi][:, :].rearrange("p (a b) -> p a b", a=8)[:, :, 0:M]
                if half == 0:
                    S.op('act', lambda e, dst=dst, src=src: e.copy(out=dst, in_=src), r=[f'pT{pi}'], w=['x1T'])
                else:
                    S.op('dve', lambda e, dst=dst, src=src: e.tensor_copy(dst, src), r=[f'pT{pi}'], w=['x1T'])

        def ffn_load(s_):
            slot = nxt('w', 2)
            Wv_ = Wb[slot][:, :].rearrange("p (g k c) -> p g k c", g=2, k=KT)
            wkey_ = f'W{slot}'
            for gi, wsrc in enumerate((w_gate, w_up)):
                S.dma('pool', wkey_, lambda e, Wv_=Wv_, gi=gi, wsrc=wsrc: e.dma_start(out=Wv_[:, gi, :, :], in_=wsrc[:, s_ * 256:(s_ + 1) * 256].rearrange("(kt p) c -> p kt c", p=128)), w=[wkey_])
            return Wv_, wkey_
        ffn_pre = {}
        for ti in range(9):
            r0, M = tile_rows(ti)
            xi = nxt('xp', 2)
            pi_ = nxt('x1p', 3)
            xk = f'x1p{pi_}'
            src = x_own[r0:r0 + M, :] if ti < 8 else x_s
            S.dma('sp', f'xp{xi}', lambda e, xi=xi, M=M, src=src: e.dma_start(out=xps[xi][0:M, :], in_=src), w=[f'xp{xi}'])
            lf = (lambda r0, M: (lambda kc: mixTa[:, kc, r0:r0 + M] if kc < 12 else mixTc[:, kc - 12, r0:r0 + M]))(r0, M)
            for cb in range(4):
                acc, akey = proj_tm(lf, M, Wo[cb][0], Wo[cb][1], ['mixTa', 'mixTc'])
                S.op('dve', lambda e, xi=xi, M=M, pi_=pi_, cb=cb, acc=acc: e.scalar_tensor_tensor(x1ps[pi_][0:M, cb * 512:(cb + 1) * 512], xps[xi][0:M, cb * 512:(cb + 1) * 512], ALPHA, acc[0:M, :], op0=ALU.mult, op1=ALU.add),
                     r=[akey, f'xp{xi}'], w=[xk])
            if ti == 8:
                ffn_pre[0] = ffn_load(0)
                ffn_pre[1] = ffn_load(1)
            bi = nxt('x1b', 2)
            flush_deferred(0)
            layernorm(x1ps[pi_][0:M, :], M, lng, lnb, x1bs[bi], xk, jkey=f'x1b{bi}', beta_eng='dve')

            def stage2(bi=bi, r0=r0, M=M, pi_=pi_, xk=xk, ti=ti):
                S.dma('pool', f'x1st{pi_}', lambda e: e.dma_start(out=x1s[r0:r0 + M, :], in_=x1ps[pi_][0:M, :]), r=[xk], w=[f'x1s{ti}'])
                S.op('act', lambda e: e.copy(out=x1bs[bi][0:M, :], in_=x1ps[pi_][0:M, :]), r=[xk], w=[f'x1b{bi}'])
                x1_transposes(bi, r0, M)
            deferred.append(stage2)
        flush_deferred(0)

        S.barrier()
        if stop == 'wout':
            S.emit(st)
            return nc

        fT = buf('fT', [NFC, 1152], BF16)
        sgf = [buf(f'sgf{i}', [512], F32) for i in range(2)]
        ctr['sgf'] = 0
        for s in range(22):
            Wv, wkey = ffn_pre.pop(s) if s in ffn_pre else ffn_load(s)
            for ch in range(2):
                fc = 2 * s + ch
                for (c0, N) in ((0, 512), (512, 512), (TOK, 4)):
                    ag = nxt('acc', 4)
                    au = nxt('acc', 4)
                    for (gi, ai) in ((0, ag), (1, au)):
                        for kt in range(KT):
                            S.op('pe', lambda e, gi=gi, ai=ai, kt=kt, Wv=Wv, ch=ch, c0=c0, N=N: e.matmul(accs[ai][:, 0:N], lhsT=Wv[:, gi, kt, ch * 128:(ch + 1) * 128], rhs=x1T[:, kt, c0:c0 + N], start=(kt == 0), stop=(kt == KT - 1)),
                                 r=['x1T', wkey], w=[f'acc{ai}'])
                    si = nxt('sgf', 2)
                    S.op('act', lambda e, si=si, ag=ag, N=N: e.activation(out=sgf[si][:, 0:N], in_=accs[ag][:, 0:N], func=AF.Silu), r=[f'acc{ag}'], w=[f'sgf{si}'])
                    S.op('dve', lambda e, si=si, au=au, N=N, fc=fc, c0=c0: e.tensor_tensor(fT[:, fc, c0:c0 + N], sgf[si][:, 0:N], accs[au][:, 0:N], op=ALU.mult), r=[f'sgf{si}', f'acc{au}'], w=['fT'])
        S.op('dve', lambda e: e.memset(fT[:, :, TS:1152], 0.0), w=['fTpad'])
        def wdn_src(cb_, half_):
            return w_down[half_ * 2816:(half_ + 1) * 2816, cb_ * 256:(cb_ + 1) * 256].rearrange("(fc p) c -> p fc c", p=128)
        wdn_pre = {(0, 0): load_w(wdn_src(0, 0), (22, 256)), (0, 1): load_w(wdn_src(0, 1), (22, 256))}
        S.barrier()
        if stop == 'ffn':
            S.emit(st)
            return nc

        dxb = [buf(f'dxb{i}', [9, 256], F32) for i in range(2)]
        lng2 = buf('lng2', [D], F32)
        lnb2 = buf('lnb2', [D], F32)
        S.dma('sp', 'lnp2', lambda e: e.dma_start(out=lng2, in_=lnp[2].partition_broadcast(128)), w=['lnp'])
        S.dma('sp', 'lnp2', lambda e: e.dma_start(out=lnb2, in_=lnp[3].partition_broadcast(128)), w=['lnp'])
        for ti in range(9):
            r0, M = tile_rows(ti)
            S.dma('sp', f'ybeta{ti}', lambda e, r0=r0, M=M: e.dma_start(out=y_out[r0:r0 + M, :], in_=lnb2[0:M, :]), r=['lnp'], w=[f'yout{ti}'])
        ls1 = buf('ls1', [9, 8], F32)
        ls2 = buf('ls2', [9, 8], F32)
        ljunk = buf('ljunk', [256], BF16)
        ljunk2 = buf('ljunk2', [256], BF16)
        S.op('dve', lambda e: e.memset(ls1, 0.0), w=['lsa'])
        S.op('dve', lambda e: e.memset(ls2, 0.0), w=['ls'])
        dyb = [buf(f'dyb{i}', [9, 256], F32) for i in range(2)]

        def dbank(ti):
            return (accs[ti // 2], (ti % 2) * 256) if ti < 8 else (pSs[0], 0)

        x1keys = [f'x1s{ti}' for ti in range(9)]
        for cb in range(8):
            bi = cb % 2
            cs = slice(cb * 256, (cb + 1) * 256)
            S.dma('sp', f'dxb{bi}', lambda e, bi=bi, cs=cs: e.dma_start(out=dxb[bi][:, 0:8, :], in_=x1s[0:TOK, cs].rearrange("(t p) c -> p t c", p=128)), r=x1keys, w=[f'dxb{bi}'])
            S.dma('sp', f'dxb{bi}', lambda e, bi=bi, cs=cs: e.dma_start(out=dxb[bi][0:4, 8, :], in_=x1s[TOK:TS, cs]), r=x1keys, w=[f'dxb{bi}'])
            for half in range(2):
                Wv, wkey = wdn_pre.pop((cb, half)) if (cb, half) in wdn_pre else load_w(wdn_src(cb, half), (22, 256))
                for j in range(22):
                    fc = half * 22 + j
                    for ti in range(9):
                        r0, M = tile_rows(ti)
                        bank, colo = dbank(ti)
                        first = (fc == 0) and (ti % 2 == 0)
                        S.op('pe', lambda e, bank=bank, colo=colo, fc=fc, r0=r0, Wv=Wv, j=j, first=first: e.matmul(bank[:, colo:colo + 256], lhsT=fT[:, fc, r0:r0 + 128], rhs=Wv[:, j, :], start=first, stop=(fc == NFC - 1), skip_group_check=True),
                             r=['fT', wkey], w=[f'dacc{ti // 2}'])
            for ti in range(9):
                r0, M = tile_rows(ti)
                bank, colo = dbank(ti)
                S.op('dve', lambda e, bi=bi, ti=ti, M=M, bank=bank, colo=colo: e.scalar_tensor_tensor(dyb[bi][0:M, ti, :], dxb[bi][0:M, ti, :], ALPHA, bank[0:M, colo:colo + 256], op0=ALU.mult, op1=ALU.add),
                     r=[f'dacc{ti // 2}', f'dxb{bi}'], w=[f'dyb{bi}_{ti}'])
            for ti in range(9):
                r0, M = tile_rows(ti)
                S.op('dve', lambda e, bi=bi, ti=ti, M=M, cb=cb: e.tensor_scalar(ljunk2[0:M, :], dyb[bi][0:M, ti, :], 1.0, 0.0, op0=ALU.mult, op1=ALU.add, accum_out=ls1[0:M, ti, cb:cb + 1]), r=[f'dyb{bi}_{ti}', 'lsa'], w=[f'lsa{ti}_{cb}', 'ljunk2'])
                S.op('act', lambda e, bi=bi, ti=ti, M=M, cb=cb: e.activation(out=ljunk[0:M, :], in_=dyb[bi][0:M, ti, :], func=AF.Square, accum_out=ls2[0:M, ti, cb:cb + 1]), r=[f'dyb{bi}_{ti}', 'ls'], w=[f'ls{ti}_{cb}', 'ljunk'])
            S.dma('sp', f'dyb{bi}', lambda e, bi=bi, cs=cs: e.dma_start(out=y2s[0:TOK, cs].rearrange("(t p) c -> p t c", p=128), in_=dyb[bi][:, 0:8, :]), r=[f'dyb{bi}_{t_}' for t_ in range(9)], w=[f'y2s{cb}'])
            S.dma('sp', f'dyb{bi}', lambda e, bi=bi, cs=cs: e.dma_start(out=y2s[TOK:TS, cs], in_=dyb[bi][0:4, 8, :]), r=[f'dyb{bi}_{t_}' for t_ in range(9)], w=[f'y2s{cb}'])
        S.barrier()
        if stop == 'wdn':
            S.emit(st)
            return nc

        yts = [buf(f'yt{i}', [D], F32) for i in range(5)]
        lnjunk2 = buf('lnjunk2', [D], F32)
        ctr['yt'] = 0
        st9 = buf('stat', [18, 8], F32)[:, 0:8, :].rearrange("p a b -> p (a b)")
        sA, sB, mean9, var9, rstd9, nb9 = (st9[:, 9 * i:9 * i + 9] for i in range(6))
        S.op('dve', lambda e: e.reduce_sum(out=sA, in_=ls1, axis=mybir.AxisListType.X), w=['st9'])
        S.op('dve', lambda e: e.reduce_sum(out=sB, in_=ls2, axis=mybir.AxisListType.X), r=['st9'], w=['st9'])
        S.op('dve', lambda e: e.tensor_scalar(mean9, sA, 1.0 / D, None, op0=ALU.mult), r=['st9'], w=['st9'])
        S.op('dve', lambda e: e.tensor_tensor(var9, mean9, mean9, op=ALU.mult), r=['st9'], w=['st9'])
        S.op('dve', lambda e: e.scalar_tensor_tensor(var9, sB, 1.0 / D, var9, op0=ALU.mult, op1=ALU.subtract), r=['st9'], w=['st9'])
        S.op('dve', lambda e: e.tensor_scalar(var9, var9, EPS, None, op0=ALU.add), r=['st9'], w=['st9'])
        S.op('act', lambda e: e.activation(out=var9, in_=var9, func=AF.Ln), r=['st9'], w=['st9'])
        S.op('act', lambda e: e.activation(out=rstd9, in_=var9, func=AF.Exp, scale=-0.5), r=['st9'], w=['st9'])
        S.op('dve', lambda e: e.scalar_tensor_tensor(nb9, mean9, -1.0, rstd9, op0=ALU.mult, op1=ALU.mult), r=['st9'], w=['st9'])
        for ti in range(9):
            r0, M = tile_rows(ti)
            yi = nxt('yt', 5)
            xap = yts[yi][0:M, :]
            S.dma('sp', f'yt{yi}', lambda e, yi=yi, r0=r0, M=M: e.dma_start(out=yts[yi][0:M, :], in_=y2s[r0:r0 + M, :]), r=[f'y2s{cb}' for cb in range(8)], w=[f'yt{yi}'])
            S.op('act', lambda e, xap=xap, ti=ti, M=M: e.activation(out=xap, in_=xap, func=AF.Identity, scale=rstd9[0:M, ti:ti + 1], bias=nb9[0:M, ti:ti + 1]), r=[f'yt{yi}', 'st9'], w=[f'yt{yi}'])
            S.op('dve', lambda e, xap=xap, M=M: e.tensor_tensor(xap, xap, lng2[0:M, :], op=ALU.mult), r=[f'yt{yi}', 'lnp'], w=[f'yt{yi}'])
            S.dma('pool', f'yto{yi}', lambda e, yi=yi, r0=r0, M=M: e.dma_start(out=y_out[r0:r0 + M, :], in_=yts[yi][0:M, :], accum_op=ALU.add), r=[f'yt{yi}', f'yout{ti}'])
        S.emit(st)
    return nc


def _rope_tables(T0):
    half = 64
    inv = (10000.0 ** (-np.arange(half, dtype=np.float32) / np.float32(half))).astype(np.float32)
    p = np.arange(128)
    pos = np.zeros((NTAB, 128), dtype=np.int64)
    for n in range(8):
        pos[TAB_G[0] + n] = T0 + 128 * n + p
    pos[TAB_G[0] + 8] = T0 - 128 + p
    for r4 in range(4):
        for n in range(2):
            pos[TAB_G[1] + r4 * 2 + n] = T0 + r4 + 4 * (128 * n + p)
        pos[TAB_G[1] + 8 + r4] = T0 - 512 + r4 + 4 * p
    for t in range(8):
        pos[TAB_G[2] + t] = T0 + t + 8 * p
    for r in range(16):
        pos[TAB_G[2] + 8 + r] = T0 - 2048 + r + 16 * p
    pos[TAB_S] = PAST + np.minimum(p, 3)
    pos = np.maximum(pos, 0)
    ang = pos.astype(np.float32)[:, :, None] * inv[None, None, :]
    cos = np.cos(ang).astype(np.float32)
    sin = np.sin(ang).astype(np.float32)
    tab = np.concatenate([cos, cos, -sin, sin], axis=-1).astype(np.float32)
    return np.ascontiguousarray(tab)


def _masks(j):
    m = np.full((128, NMASK, 128), NEG, dtype=np.float32)
    b = np.arange(128)[:, None]
    a = np.arange(128)[None, :]
    m[:, 0, :] = np.where(b >= a, 0.0, NEG)
    m[:, 1, :] = np.where(b <= a, 0.0, NEG)
    m[:, 2, :] = m[:, 0, :] if j > 0 else NEG
    m[:, 3, :] = np.where(((b % 2) == (a % 2)) & ((b // 2) <= (a // 2)), 0.0, NEG)
    valid = b >= (a // 2)
    if j == 0:
        valid = valid & False
    elif j == 1:
        valid = valid & (b >= 64)
    m[:, 4, :] = np.where(valid & (a % 2 == 0), 0.0, NEG)
    m[:, 12, :] = np.where(valid & (a % 2 == 1), 0.0, NEG)
    m[:, 5, :] = np.where(b >= a, 0.0, NEG)
    m[:, 6, :] = np.where(b <= a, 0.0, NEG)
    for t in range(4):
        m[:, 7 + t, :] = np.where(a == t, 0.0, NEG) + 0.0 * b
    m[:, 11, :] = np.where(b == a, 0.0, NEG)
    return np.ascontiguousarray(m)


_NC_CACHE = {}


def kernel(x_prompt, x_sample, cache_kv_w128, cache_kv_w512, cache_kv_w2048, state_conv,
           w_in, w_out, conv_w, conv_b, conv_ln_g, conv_ln_b, ln1_g, ln1_b,
           w_gate, w_up, w_down, ln2_g, ln2_b):
    f32 = np.float32
    x_prompt = np.asarray(x_prompt, f32)
    x_sample = np.asarray(x_sample, f32)
    caches = [np.asarray(c, f32) for c in (cache_kv_w128, cache_kv_w512, cache_kv_w2048)]
    state_conv = np.asarray(state_conv, f32)
    if 'nc' not in _NC_CACHE:
        _NC_CACHE['nc'] = build_program(_NC_CACHE.get('stop'))
    nc = _NC_CACHE['nc']
    shared = {
        "w_in": np.ascontiguousarray(np.asarray(w_in, f32)[0]),
        "w_out": np.ascontiguousarray(np.asarray(w_out, f32)[0]),
        "w_gate": np.ascontiguousarray(np.asarray(w_gate, f32)[0]),
        "w_up": np.ascontiguousarray(np.asarray(w_up, f32)[0]),
        "w_down": np.ascontiguousarray(np.asarray(w_down, f32)[0]),
        "conv_wT": np.ascontiguousarray(np.asarray(conv_w, f32)[0].T),
        "cvec": np.ascontiguousarray(np.concatenate([np.asarray(v, f32)[0].reshape(4, 128).T for v in (conv_b, conv_ln_g, conv_ln_b)], axis=1)),
        "lnp": np.ascontiguousarray(np.stack([np.asarray(v, f32)[0] for v in (ln1_g, ln1_b, ln2_g, ln2_b)], axis=0)),
        "ident": np.eye(128, dtype=f32),
    }
    in_maps = []
    for c in range(NCORES):
        b, j = c // 4, c % 4
        T0 = 1024 * j
        xh = np.zeros((HALO, D), dtype=f32)
        lo = max(0, T0 - HALO)
        if T0 > 0:
            xh[HALO - (T0 - lo):] = x_prompt[b, lo:T0]
        m = dict(shared)
        m["x_own"] = np.ascontiguousarray(x_prompt[b, T0:T0 + TOK])
        m["x_halo"] = xh
        m["x_s"] = np.ascontiguousarray(x_sample[c])
        for g in range(3):
            m[f"kvc{g}"] = np.ascontiguousarray(caches[g][0, c].reshape(-1, 2, 512))
        m["sconv"] = np.ascontiguousarray(state_conv[0, c])
        m["masks"] = _masks(j)
        m["rope"] = _rope_tables(T0)
        in_maps.append(m)
    res = run_bass_kernel_spmd(nc, in_maps, core_ids=list(range(NCORES)))
    R = res.results
    y_prompt = np.zeros((2, 4096, D), f32)
    y_sample = np.zeros((8, 4, D), f32)
    for c in range(NCORES):
        b, j = c // 4, c % 4
        y_prompt[b, 1024 * j:1024 * (j + 1)] = R[c]["y"][0:TOK]
        y_sample[c] = R[c]["y"][TOK:TS]
    kvp = []
    for g, keep in enumerate((128, 512, 2048)):
        out = np.zeros((1, 2, keep, 2, 4, 128), f32)
        for b in range(2):
            if keep <= 1024:
                out[0, b] = R[4 * b + 3][f"kvout{g}"][TOK - keep:TOK].reshape(keep, 2, 4, 128)
            else:
                out[0, b, 0:1024] = R[4 * b + 2][f"kvout{g}"].reshape(TOK, 2, 4, 128)
                out[0, b, 1024:2048] = R[4 * b + 3][f"kvout{g}"].reshape(TOK, 2, 4, 128)
        kvp.append(out)
    convp = np.stack([R[4 * b + 3]["convout"] for b in range(2)], axis=0)[None].astype(f32)
    kvs = []
    for g, nb in enumerate((128, 512, 2048)):
        kvs.append(np.stack([R[c][f"kvs{g}"].reshape(nb, 2, 4, 128) for c in range(NCORES)], axis=0)[None].astype(f32))
    convs = np.stack([R[c]["convs"] for c in range(NCORES)], axis=0)[None].astype(f32)
    return (y_prompt, y_sample, kvp[0], kvp[1], kvp[2], convp, kvs[0], kvs[1], kvs[2], convs)
```
